# Optimizing a Trainium2 kernel written in Bass

```python
import jax, jax.numpy as jnp
from jax import lax
import numpy as np

D_MODEL = 1024
BATCH = 4
SEQ = 8192
DEPTH = 2

CHUNK = 64
Q_BLOCK = 128
NORM_EPS = 1e-6

SB_HEADS = 8
SB_HEAD_DIM = 64
SB_WIDTH = SB_HEADS * SB_HEAD_DIM
POOL_WINDOWS = (2, 4, 8, 16)
POOL_GROUPS = len(POOL_WINDOWS)
POOL_WIDTH = D_MODEL - SB_WIDTH
POOL_GROUP_DIM = POOL_WIDTH // POOL_GROUPS
EVEN_IN_WIDTH = 3 * SB_WIDTH + POOL_WIDTH
EVEN_MIX_WIDTH = SB_WIDTH + POOL_WIDTH

GLA_HEADS = 4
GLA_KEY_WIDTH = D_MODEL // 2
GLA_VALUE_WIDTH = D_MODEL
GLA_KEY_DIM = GLA_KEY_WIDTH // GLA_HEADS
GLA_VALUE_DIM = GLA_VALUE_WIDTH // GLA_HEADS
GLA_GATE_RANK = 16
GLA_TAU = 16.0
ODD_IN_WIDTH = 2 * GLA_KEY_WIDTH + 2 * GLA_VALUE_WIDTH + GLA_GATE_RANK

FFN_HIDDEN = 2816
CONV_WIDTH = 3

N_EVEN = (DEPTH + 1) // 2
N_ODD = DEPTH // 2

kernel_name = "hybrid_stickbreak_pool_gla_convffn"


def rms_norm(x, gain):
    xf = x.astype(jnp.float32)
    y = xf * lax.rsqrt(jnp.mean(xf * xf, axis=-1, keepdims=True) + NORM_EPS)
    return (y * gain.astype(jnp.float32)).astype(x.dtype)


def stick_breaking_attention(q, k, v):
    b, s, h, dh = q.shape
    qh = jnp.transpose(q, (0, 2, 1, 3))
    kh = jnp.transpose(k, (0, 2, 1, 3))
    vh = jnp.transpose(v, (0, 2, 1, 3))
    scale = dh ** -0.5
    key_pos = jnp.arange(s)

    def block(i):
        start = i * Q_BLOCK
        q_blk = lax.dynamic_slice_in_dim(qh, start, Q_BLOCK, axis=2)
        z = jnp.einsum('bhqd,bhkd->bhqk', q_blk, kh,
                       preferred_element_type=jnp.float32) * scale
        q_pos = start + jnp.arange(Q_BLOCK)
        visible = key_pos[None, :] < q_pos[:, None]
        log_fail = jnp.where(visible, jax.nn.log_sigmoid(-z), 0.0)
        cum_rev = lax.cumsum(log_fail, axis=3, reverse=True)
        later = jnp.concatenate([cum_rev[..., 1:], jnp.zeros_like(cum_rev[..., :1])], axis=-1)
        weights = jnp.where(visible, jnp.exp(jax.nn.log_sigmoid(z) + later), 0.0)
        return jnp.einsum('bhqk,bhkd->bhqd', weights.astype(vh.dtype), vh)

    out = lax.map(block, jnp.arange(s // Q_BLOCK))
    return jnp.transpose(out, (1, 0, 3, 2, 4)).reshape(b, s, h * dh)


def multiscale_pool(xb, pool_w, pool_scale):
    b, s, _ = xb.shape
    groups = xb.reshape(b, s, POOL_GROUPS, POOL_GROUP_DIM)
    prefix = jnp.cumsum(groups.astype(jnp.float32), axis=1)
    prefix = jnp.pad(prefix, ((0, 0), (1, 0), (0, 0), (0, 0)))
    pos = jnp.arange(1, s + 1, dtype=jnp.float32)
    outs = []
    for g, w in enumerate(POOL_WINDOWS):
        pg = jnp.pad(prefix[:, :, g], ((0, 0), (w - 1, 0), (0, 0)))
        window_sum = pg[:, w:] - pg[:, :s]
        count = jnp.minimum(pos, float(w))[None, :, None]
        pooled = window_sum / count - groups[:, :, g].astype(jnp.float32)
        outs.append(jnp.einsum('bsc,cd->bsd', pooled.astype(xb.dtype), pool_w[g]))
    return jnp.concatenate(outs, axis=-1) * pool_scale


def gated_linear_attention(q, k, v, log_alpha):
    b, s, h, dk = q.shape
    dv = v.shape[-1]
    n = s // CHUNK

    def chunks(t):
        return jnp.moveaxis(t.reshape(b, n, CHUNK, h, t.shape[-1]), 1, 0)

    qc = chunks(q.astype(jnp.float32))
    kc = chunks(k.astype(jnp.float32))
    vc = chunks(v.astype(jnp.float32))
    cum = jnp.cumsum(chunks(log_alpha), axis=2)
    total = cum[:, :, -1]
    k_dec = kc * jnp.exp(total[:, :, None] - cum)
    chunk_decay = jnp.exp(total)

    def step(state, inp):
        q_c, k_c, v_c, decay_c = inp
        state = decay_c[..., None] * state + jnp.einsum('bchk,bchv->bhkv', k_c, v_c)
        return state, jnp.einsum('bchk,bhkv->bchv', q_c, state)

    state0 = jnp.zeros((b, h, dk, dv), jnp.float32)
    _, out = lax.scan(step, state0, (qc, k_dec, vc, chunk_decay))
    return jnp.moveaxis(out, 0, 1).reshape(b, s, h, dv)


def even_mixer(h, w_in, q_gain, k_gain, pool_w, pool_scale, w_out):
    b, s, _ = h.shape
    proj = h @ w_in
    q, k, v, xb = jnp.split(proj, [SB_WIDTH, 2 * SB_WIDTH, 3 * SB_WIDTH], axis=-1)
    q = rms_norm(q.reshape(b, s, SB_HEADS, SB_HEAD_DIM), q_gain)
    k = rms_norm(k.reshape(b, s, SB_HEADS, SB_HEAD_DIM), k_gain)
    v = v.reshape(b, s, SB_HEADS, SB_HEAD_DIM)
    o_a = stick_breaking_attention(q, k, v)
    o_b = multiscale_pool(xb, pool_w, pool_scale)
    return jnp.concatenate([o_a.astype(h.dtype), o_b.astype(h.dtype)], axis=-1) @ w_out


def odd_mixer(h, w_in, w_a2, b_a, o_gain, w_out):
    b, s, _ = h.shape
    proj = h @ w_in
    q, k, v, r, a_low = jnp.split(
        proj, [GLA_KEY_WIDTH, 2 * GLA_KEY_WIDTH, 2 * GLA_KEY_WIDTH + GLA_VALUE_WIDTH,
               2 * GLA_KEY_WIDTH + 2 * GLA_VALUE_WIDTH], axis=-1)
    q = q.reshape(b, s, GLA_HEADS, GLA_KEY_DIM) * (GLA_KEY_DIM ** -0.5)
    k = k.reshape(b, s, GLA_HEADS, GLA_KEY_DIM)
    v = v.reshape(b, s, GLA_HEADS, GLA_VALUE_DIM)
    gate_logit = (a_low @ w_a2 + b_a).astype(jnp.float32).reshape(b, s, GLA_HEADS, GLA_KEY_DIM)
    log_alpha = jax.nn.log_sigmoid(gate_logit) / GLA_TAU
    o = gated_linear_attention(q, k, v, log_alpha)
    o = rms_norm(o, o_gain).reshape(b, s, GLA_VALUE_WIDTH).astype(h.dtype)
    return (o * jax.nn.silu(r)) @ w_out


def conv_ffn(h, w_up, conv_w, conv_b, w_down):
    s = h.shape[1]
    u = h @ w_up
    up = jnp.pad(u, ((0, 0), (CONV_WIDTH - 1, 0), (0, 0)))
    conv = conv_b
    for j in range(CONV_WIDTH):
        conv = conv + conv_w[j] * up[:, j:j + s]
    a, g = jnp.split(conv, 2, axis=-1)
    return (jax.nn.silu(a) * g) @ w_down


def setup_inputs(seed: int = 0) -> dict:
    key = jax.random.key(seed)
    ks = jax.random.split(key, 20)
    f32 = jnp.float32

    def nrm(k, shape, scale):
        return jax.random.normal(k, shape, f32) * scale

    return {
        "x": nrm(ks[0], (BATCH, SEQ, D_MODEL), 1.0),
        "mix_norm_even": 1.0 + nrm(ks[1], (N_EVEN, D_MODEL), 0.02),
        "w_in_even": nrm(ks[2], (N_EVEN, D_MODEL, EVEN_IN_WIDTH), D_MODEL ** -0.5),
        "sb_q_gain": 1.0 + nrm(ks[3], (N_EVEN, SB_HEAD_DIM), 0.02),
        "sb_k_gain": 1.0 + nrm(ks[4], (N_EVEN, SB_HEAD_DIM), 0.02),
        "pool_w": nrm(ks[5], (N_EVEN, POOL_GROUPS, POOL_GROUP_DIM, POOL_GROUP_DIM), POOL_GROUP_DIM ** -0.5),
        "pool_scale": 1.0 + nrm(ks[6], (N_EVEN, POOL_WIDTH), 0.1),
        "w_out_even": nrm(ks[7], (N_EVEN, EVEN_MIX_WIDTH, D_MODEL), EVEN_MIX_WIDTH ** -0.5),
        "mix_norm_odd": 1.0 + nrm(ks[8], (N_ODD, D_MODEL), 0.02),
        "w_in_odd": nrm(ks[9], (N_ODD, D_MODEL, ODD_IN_WIDTH), D_MODEL ** -0.5),
        "gla_w_a2": nrm(ks[10], (N_ODD, GLA_GATE_RANK, GLA_KEY_WIDTH), GLA_GATE_RANK ** -0.5),
        "gla_b_a": nrm(ks[11], (N_ODD, GLA_KEY_WIDTH), 0.01),
        "gla_o_gain": 1.0 + nrm(ks[12], (N_ODD, GLA_VALUE_DIM), 0.02),
        "w_out_odd": nrm(ks[13], (N_ODD, GLA_VALUE_WIDTH, D_MODEL), GLA_VALUE_WIDTH ** -0.5),
        "ffn_norm": 1.0 + nrm(ks[14], (DEPTH, D_MODEL), 0.02),
        "ffn_w_up": nrm(ks[15], (DEPTH, D_MODEL, 2 * FFN_HIDDEN), D_MODEL ** -0.5),
        "ffn_conv_w": nrm(ks[16], (DEPTH, CONV_WIDTH, 2 * FFN_HIDDEN), CONV_WIDTH ** -0.5),
        "ffn_conv_b": nrm(ks[17], (DEPTH, 2 * FFN_HIDDEN), 0.01),
        "ffn_w_down": nrm(ks[18], (DEPTH, FFN_HIDDEN, D_MODEL), FFN_HIDDEN ** -0.5),
    }


def reference(x, mix_norm_even, w_in_even, sb_q_gain, sb_k_gain, pool_w, pool_scale, w_out_even,
              mix_norm_odd, w_in_odd, gla_w_a2, gla_b_a, gla_o_gain, w_out_odd,
              ffn_norm, ffn_w_up, ffn_conv_w, ffn_conv_b, ffn_w_down):
    for layer in range(DEPTH):
        if layer % 2 == 0:
            i = layer // 2
            x = x + even_mixer(rms_norm(x, mix_norm_even[i]), w_in_even[i], sb_q_gain[i],
                               sb_k_gain[i], pool_w[i], pool_scale[i], w_out_even[i])
        else:
            i = layer // 2
            x = x + odd_mixer(rms_norm(x, mix_norm_odd[i]), w_in_odd[i], gla_w_a2[i],
                              gla_b_a[i], gla_o_gain[i], w_out_odd[i])
        x = x + conv_ffn(rms_norm(x, ffn_norm[layer]), ffn_w_up[layer], ffn_conv_w[layer],
                         ffn_conv_b[layer], ffn_w_down[layer])
    return x
```

```python
from contextlib import ExitStack
import numpy as np
import ml_dtypes
import concourse.bass as bass
import concourse.mybir as mybir
from concourse.bass_utils import run_bass_kernel_spmd

F32 = mybir.dt.float32
BF16 = mybir.dt.bfloat16
AF = mybir.ActivationFunctionType
ALU = mybir.AluOpType
AX = mybir.AxisListType

D = 1024
FH = 2816
EPS = 1e-6
SAME_ENGINE_SYNC = True


class Res:
    __slots__ = ("name", "w", "rs", "sem", "dcnt")

    def __init__(self, name):
        self.name = name
        self.w = None
        self.rs = []
        self.sem = None
        self.dcnt = 0


class Eng:
    def __init__(self, name, sem):
        self.name = name
        self.sem = sem
        self.count = 0
        self.seen = {}
        self.ops = []


class SemPool:
    def __init__(self, nc, es):
        self.nc, self.es, self.n = nc, es, 0

    def pop(self):
        self.n += 1
        return self.es.enter_context(self.nc.semaphore(f"sem{self.n}"))


class Sched:
    ENGS = ("pe", "act", "dve", "pool", "sp")

    def __init__(self, nc, sem_pool):
        self.nc = nc
        self.sem_pool = sem_pool
        self.engs = {n: Eng(n, sem_pool.pop()) for n in self.ENGS}
        self.slots = []

    def res(self, name):
        return Res(name)

    def slot(self, name):
        r = Res(name)
        r.sem = self.sem_pool.pop()
        self.slots.append(r)
        return r

    def _need(self, eng, ev, waits):
        kind, key, val = ev
        if kind == "e":
            if key is eng:
                if eng.name in ("pe", "sp") or not SAME_ENGINE_SYNC:
                    return
            sem = key.sem
        else:
            sem = key.sem
        k = id(sem)
        if eng.seen.get(k, 0) >= val:
            return
        eng.seen[k] = val
        waits.append((sem, val))

    def op(self, engname, fn, reads=(), writes=(), dma=None):
        eng = self.engs[engname]
        waits = []
        for r in reads:
            if r.w is not None:
                self._need(eng, r.w, waits)
        for w in writes:
            if w.w is not None:
                self._need(eng, w.w, waits)
            for ev in w.rs:
                self._need(eng, ev, waits)
        if dma is None:
            eng.count += 1
            ev = ("e", eng, eng.count)
            inc = (eng.sem, 1)
        else:
            dma.dcnt += 1
            ev = ("d", dma, 16 * dma.dcnt)
            inc = (dma.sem, 16)
        for r in reads:
            r.rs.append(ev)
        for w in writes:
            w.w = ev
            w.rs = []
        eng.ops.append((waits, fn, inc))

    def pe(self, fn, reads=(), writes=()):
        self.op("pe", fn, reads, writes)

    def act(self, fn, reads=(), writes=()):
        self.op("act", fn, reads, writes)

    def dve(self, fn, reads=(), writes=()):
        self.op("dve", fn, reads, writes)

    def pool(self, fn, reads=(), writes=()):
        self.op("pool", fn, reads, writes)

    def dma(self, q, fn, slot, reads=(), writes=()):
        self.op(q, fn, reads, writes, dma=slot)

    def emit(self):
        finals = []
        for e in self.engs.values():
            if e.count > 0:
                finals.append((e.sem, e.count))
        for s in self.slots:
            if s.dcnt > 0:
                finals.append((s.sem, 16 * s.dcnt))
        nc = self.nc
        with nc.Block() as block:
            def run(e, eng):
                for waits, fn, inc in eng.ops:
                    for sem, val in waits:
                        e.wait_ge(sem, val)
                    ins = fn(e)
                    ins.then_inc(inc[0], inc[1])
                for sem, val in finals:
                    e.wait_ge(sem, val)

            @block.tensor
            def _(e):
                run(e, self.engs["pe"])

            @block.scalar
            def _(e):
                run(e, self.engs["act"])

            @block.vector
            def _(e):
                run(e, self.engs["dve"])

            @block.gpsimd
            def _(e):
                run(e, self.engs["pool"])

            @block.sync
            def _(e):
                run(e, self.engs["sp"])


def sb(es, nc, name, shape, dt):
    return es.enter_context(nc.sbuf_tensor(name, list(shape), dt))


def ps(es, nc, name, shape, dt):
    return es.enter_context(nc.psum_tensor(name, list(shape), dt))


def load_w_bf16(s, W_sb, w_dram, K, slot, res):
    for c in range(K // 128):
        s.dma("pool", lambda e, c=c: e.dma_start(out=W_sb[:, c, :], in_=w_dram[c * 128:(c + 1) * 128, :]),
              slot, writes=[res])


def rms_rstd(s, ss, lnv, rstd, n, r_ss, r_ln, r_rstd):
    s.act(lambda e: e.activation(out=lnv, in_=ss, func=AF.Ln, bias=EPS, scale=1.0 / n),
          reads=[r_ss], writes=[r_ln])
    s.act(lambda e: e.activation(out=rstd, in_=lnv, func=AF.Exp, scale=-0.5),
          reads=[r_ln], writes=[r_rstd])


class NormCtx:
    def __init__(self, es, nc, s, tag, gain_dram, ident):
        self.nc, self.s = nc, s
        self.junk = sb(es, nc, tag + "junk", [128, D], BF16)
        self.gain = sb(es, nc, tag + "gain", [128, D], F32)
        self.ss = [sb(es, nc, f"{tag}ss{i}", [128, 1], F32) for i in range(2)]
        self.lnv = [sb(es, nc, f"{tag}lnv{i}", [128, 1], F32) for i in range(2)]
        self.rstd = [sb(es, nc, f"{tag}rstd{i}", [128, 1], F32) for i in range(2)]
        self.hb = [sb(es, nc, f"{tag}hb{i}", [128, D], BF16) for i in range(2)]
        self.ident = ident
        self.r_junk, self.r_gain = s.res(tag + "junk"), s.res(tag + "gain")
        self.r_ss = [s.res(f"{tag}ss{i}") for i in range(2)]
        self.r_ln = [s.res(f"{tag}ln{i}") for i in range(2)]
        self.r_rstd = [s.res(f"{tag}rstd{i}") for i in range(2)]
        self.r_hb = [s.res(f"{tag}hb{i}") for i in range(2)]
        gslot = s.slot(tag + "gslot")
        s.dma("sp", lambda e: e.dma_start(out=self.gain[:], in_=gain_dram.broadcast_to([128, D])),
              gslot, writes=[self.r_gain])

    def part1(self, k, xt, r_xt):
        s = self.s
        k = k % 2
        s.act(lambda e: e.activation(out=self.junk[:], in_=xt, func=AF.Square, accum_out=self.ss[k][:]),
              reads=[r_xt], writes=[self.r_junk, self.r_ss[k]])
        rms_rstd(s, self.ss[k][:], self.lnv[k][:], self.rstd[k][:], D, self.r_ss[k], self.r_ln[k], self.r_rstd[k])
        s.dve(lambda e: e.scalar_tensor_tensor(out=self.hb[k][:], in0=xt, scalar=self.rstd[k][:, 0:1], in1=self.gain[:],
                                               op0=ALU.mult, op1=ALU.mult),
              reads=[r_xt, self.r_rstd[k], self.r_gain], writes=[self.r_hb[k]])

    def part2(self, k, psT, r_psT, hT_dst, r_hT):
        s = self.s
        k = k % 2
        for c in range(8):
            s.pe(lambda e, c=c: e.transpose(out=psT[:, c, :], in_=self.hb[k][:, c * 128:(c + 1) * 128],
                                            identity=self.ident[:]),
                 reads=[self.r_hb[k]], writes=[r_psT])
        s.act(lambda e: e.copy(out=hT_dst, in_=psT[:]), reads=[r_psT], writes=[r_hT])

    def run(self, xt, r_xt, psT, r_psT, hT_dst, r_hT):
        self.part1(0, xt, r_xt)
        self.part2(0, psT, r_psT, hT_dst, r_hT)


def phase_A(nc, sem_pool, S, x_d, prm, cst, qT_d, kT_d, v_d, mixT_d):
    NT = S // 128
    with ExitStack() as es:
        s = Sched(nc, sem_pool)
        ident = sb(es, nc, "A_ident", [128, 128], BF16)
        W = sb(es, nc, "A_W", [128, 8, 2048], BF16)
        PW = sb(es, nc, "A_PW", [128, 4, 128], BF16)
        PT = sb(es, nc, "A_PT", [128, 12, 128], BF16)
        qg = sb(es, nc, "A_qg", [128, 512], F32)
        kg = sb(es, nc, "A_kg", [128, 512], F32)
        psc = sb(es, nc, "A_psc", [128, 4], F32)
        r_const = s.res("const")
        cslot = s.slot("cslot")
        wslot = s.slot("wslot")
        r_W = s.res("W")
        s.dma("pool", lambda e: e.dma_start(out=ident[:], in_=cst["ident"][:, :]), cslot, writes=[r_const])
        s.dma("pool", lambda e: e.dma_start(out=PT[:], in_=cst["poolT"][:, :, :]), cslot, writes=[r_const])
        s.dma("pool", lambda e: e.dma_start(out=PW[:], in_=prm["pool_w"].rearrange("g c d -> c g d")),
              cslot, writes=[r_const])
        s.dma("pool", lambda e: e.dma_start(out=qg[:], in_=prm["qg_t"].broadcast_to([128, 512])), cslot,
              writes=[r_const])
        s.dma("pool", lambda e: e.dma_start(out=kg[:], in_=prm["kg_t"].broadcast_to([128, 512])), cslot,
              writes=[r_const])
        s.dma("pool", lambda e: e.dma_start(out=psc[:], in_=prm["pool_scale"][:, :]),
              cslot, writes=[r_const])
        load_w_bf16(s, W, prm["w_in_even"], 1024, wslot, r_W)
        s.dve(lambda e: e.tensor_scalar(out=qg[:], in0=qg[:], scalar1=0.125, scalar2=None, op0=ALU.mult),
              reads=[r_const], writes=[r_const])

        nrm = NormCtx(es, nc, s, "A_n", prm["mix_norm_even"], ident)
        xt = [sb(es, nc, f"A_xt{i}", [128, D], F32) for i in range(2)]
        xslot = [s.slot(f"xslot{i}") for i in range(2)]
        hT = sb(es, nc, "A_hT", [128, 8, 128], BF16)
        r_hT = s.res("hT")
        psT = ps(es, nc, "A_psT", [128, 8, 128], BF16)
        r_psT = s.res("psT")
        psG = [ps(es, nc, f"A_psG{g}", [128, 512], F32) for g in range(4)]
        r_psG = [s.res(f"psG{g}") for g in range(4)]
        psQK = ps(es, nc, "A_psQK", [128, 8, 128], BF16)
        r_psQK = s.res("psQK")
        psP = ps(es, nc, "A_psP", [128, 4, 128], F32)
        r_psP = s.res("psP")
        psO = ps(es, nc, "A_psO", [128, 4, 128], F32)
        r_psO = s.res("psO")

        qs = [sb(es, nc, f"A_qs{w}", [128, 512], F32) for w in range(2)]
        sq = [sb(es, nc, f"A_sq{w}", [128, 512], F32) for w in range(2)]
        ss8 = [sb(es, nc, f"A_ss8{w}", [128, 8], F32) for w in range(2)]
        ln8 = [sb(es, nc, f"A_ln8{w}", [128, 8], F32) for w in range(2)]
        r8 = [sb(es, nc, f"A_r8{w}", [128, 8], F32) for w in range(2)]
        qn = [sb(es, nc, f"A_qn{w}", [128, 2, 512], BF16) for w in range(2)]
        r_qs, r_sq, r_ss8, r_ln8, r_r8 = ([s.res(f"{n}{w}") for w in range(2)] for n in ("qs", "sq", "ss8", "ln8", "r8"))
        r_qn = [s.res(f"qn{w}") for w in range(2)]
        qTst = sb(es, nc, "A_qTst", [128, 4, 512], BF16)
        kTst = sb(es, nc, "A_kTst", [128, 4, 512], BF16)
        obst = sb(es, nc, "A_obst", [128, 4, 512], BF16)
        vst = [sb(es, nc, f"A_vst{i}", [128, 512], BF16) for i in range(2)]
        sl_q, sl_k, sl_ob = s.slot("sl_q"), s.slot("sl_k"), s.slot("sl_ob")
        sl_v = [s.slot(f"sl_v{i}") for i in range(2)]
        xb = [sb(es, nc, f"A_xb{i}", [128, 512], BF16) for i in range(2)]
        r_xb = [s.res(f"xb{i}") for i in range(2)]
        pooledT = sb(es, nc, "A_pooledT", [128, 4, 128], BF16)
        r_pooledT = s.res("pooledT")

        def a_pre(i):
            b = i % 2
            s.dma("sp", lambda e: e.dma_start(out=xt[b][:], in_=x_d[i * 128:(i + 1) * 128, :]),
                  xslot[b], writes=[xslot[b]])
            nrm.part1(i, xt[b][:], xslot[b])

        a_pre(0)
        pending = []
        for i in range(NT):
            b = i % 2
            if i + 1 < NT:
                a_pre(i + 1)
            nrm.part2(i, psT, r_psT, hT[:], r_hT)
            for g in range(4):
                for c in range(8):
                    s.pe(lambda e, g=g, c=c: e.matmul(psG[g][:], lhsT=hT[:, c, :], rhs=W[:, c, g * 512:(g + 1) * 512],
                                                      start=(c == 0), stop=(c == 7)),
                         reads=[r_hT, r_W], writes=[r_psG[g]])
            while pending:
                pending.pop(0)()
            s.act(lambda e, b=b: e.copy(out=vst[b][:], in_=psG[2][:]), reads=[r_psG[2]], writes=[sl_v[b]])
            s.dma("sp", lambda e, i=i, b=b: e.dma_start(out=v_d[i * 128:(i + 1) * 128, :], in_=vst[b][:]), sl_v[b],
                  reads=[sl_v[b]])
            s.dve(lambda e, b=b: e.tensor_copy(out=xb[b][:], in_=psG[3][:]), reads=[r_psG[3]], writes=[r_xb[b]])
            for g in range(4):
                pc = 8 + g if i == 0 else g
                s.pe(lambda e, g=g, b=b, pc=pc, i=i: e.matmul(psP[:, g, :], lhsT=xb[b][:, g * 128:(g + 1) * 128],
                                                              rhs=PT[:, pc, :], start=True, stop=(i == 0)),
                     reads=[r_xb[b], r_const], writes=[r_psP])
                if i > 0:
                    s.pe(lambda e, g=g, b=b: e.matmul(psP[:, g, :], lhsT=xb[1 - b][:, g * 128:(g + 1) * 128],
                                                      rhs=PT[:, 4 + g, :], start=False, stop=True),
                         reads=[r_xb[1 - b], r_const], writes=[r_psP])
            for which, gt in enumerate((qg, kg)):
                s.act(lambda e, which=which: e.copy(out=qs[which][:], in_=psG[which][:]), reads=[r_psG[which]],
                      writes=[r_qs[which]])
            for which in range(2):
                s.dve(lambda e, which=which: e.tensor_tensor(out=sq[which][:], in0=qs[which][:], in1=qs[which][:],
                                                             op=ALU.mult), reads=[r_qs[which]], writes=[r_sq[which]])
            for which in range(2):
                s.dve(lambda e, which=which: e.tensor_reduce(out=ss8[which][:],
                                                             in_=sq[which][:].rearrange("p (a b) -> p a b", a=8),
                                                             axis=AX.X, op=ALU.add), reads=[r_sq[which]],
                      writes=[r_ss8[which]])
            for which in range(2):
                rms_rstd(s, ss8[which][:], ln8[which][:], r8[which][:], 64, r_ss8[which], r_ln8[which], r_r8[which])
            s.act(lambda e: e.copy(out=pooledT[:], in_=psP[:]), reads=[r_psP], writes=[r_pooledT])
            for g in range(4):
                s.pe(lambda e, g=g: e.matmul(psO[:, g, :], lhsT=PW[:, g, :], rhs=pooledT[:, g, :], start=True, stop=True),
                     reads=[r_pooledT, r_const], writes=[r_psO])
            pq = i % 2
            for which, gt in enumerate((qg, kg)):
                for h in range(8):
                    s.dve(lambda e, h=h, which=which, gt=gt, pq=pq: e.scalar_tensor_tensor(
                        out=qn[pq][:, which, h * 64:(h + 1) * 64], in0=qs[which][:, h * 64:(h + 1) * 64],
                        scalar=r8[which][:, h:h + 1], in1=gt[:, h * 64:(h + 1) * 64], op0=ALU.mult, op1=ALU.mult),
                        reads=[r_qs[which], r_r8[which], r_const], writes=[r_qn[pq]])
            j = i % 4
            for g in range(4):
                s.act(lambda e, g=g, j=j: e.activation(out=obst[:, g, j * 128:(j + 1) * 128], in_=psO[:, g, :],
                                                       func=AF.Copy, scale=psc[:, g:g + 1]),
                      reads=[r_psO, r_const], writes=[sl_ob])
            if j == 3:
                t0 = (i - 3) * 128
                s.dma("sp", lambda e, t0=t0: e.dma_start(out=mixT_d[4:8, :, t0:t0 + 512].rearrange("h p t -> p h t"),
                                                         in_=obst[:]), sl_ob, reads=[sl_ob])

            def deferred(i=i, pq=pq):
                j = i % 4
                for which, (Tst, slT) in enumerate(((qTst, sl_q), (kTst, sl_k))):
                    for hp in range(4):
                        s.pe(lambda e, hp=hp, which=which: e.transpose(out=psQK[:, which * 4 + hp, :],
                                                                       in_=qn[pq][:, which, hp * 128:(hp + 1) * 128],
                                                                       identity=ident[:]),
                             reads=[r_qn[pq], r_const], writes=[r_psQK])
                for which, (Tst, slT) in enumerate(((qTst, sl_q), (kTst, sl_k))):
                    s.act(lambda e, which=which, Tst=Tst: e.copy(out=Tst[:, :, j * 128:(j + 1) * 128],
                                                                 in_=psQK[:, which * 4:(which + 1) * 4, :]),
                          reads=[r_psQK], writes=[slT])
                    if j == 3:
                        dst = qT_d if which == 0 else kT_d
                        t0 = (i - 3) * 128
                        s.dma("sp", lambda e, dst=dst, Tst=Tst, t0=t0: e.dma_start(
                            out=dst[:, :, t0:t0 + 512].rearrange("h p t -> p h t"), in_=Tst[:]), slT, reads=[slT])

            pending.append(deferred)
        while pending:
            pending.pop(0)()
        s.emit()


def phase_B(nc, sem_pool, S, cst, qT_d, kT_d, v_d, mixT_d):
    NQ = S // 512
    NB = S // 128
    with ExitStack() as es:
        s = Sched(nc, sem_pool)
        mask = sb(es, nc, "B_mask", [128, 128], BF16)
        uneg = sb(es, nc, "B_uneg", [128, 128], BF16)
        oneg = sb(es, nc, "B_oneg", [128, 128], BF16)
        r_const = s.res("const")
        cslot = s.slot("cslot")
        s.dma("pool", lambda e: e.dma_start(out=mask[:], in_=cst["mask"][:, :]), cslot, writes=[r_const])
        s.dma("pool", lambda e: e.dma_start(out=uneg[:], in_=cst["uneg"][:, :]), cslot, writes=[r_const])
        s.dma("pool", lambda e: e.dma_start(out=oneg[:], in_=cst["oneg"][:, :]), cslot, writes=[r_const])
        qT = sb(es, nc, "B_qT", [128, S], BF16)
        kTh = [sb(es, nc, f"B_kT{h}", [128, S], BF16) for h in range(2)]
        vv = sb(es, nc, "B_v", [128, NB, 128], BF16)
        vp = [sb(es, nc, f"B_vp{h}", [128, NB, 128], BF16) for h in range(2)]
        r_vp = [s.res(f"vp{h}") for h in range(2)]
        oT = sb(es, nc, "B_oT", [128, S], BF16)
        sl_qT, sl_v, sl_oT = s.slot("qT"), s.slot("v"), s.slot("oT")
        sl_kTh = [s.slot(f"kT{h}") for h in range(2)]
        s.pool(lambda e: e.memset(kTh[0][64:128, :], 0.0), writes=[sl_kTh[0]])
        s.pool(lambda e: e.memset(kTh[1][0:64, :], 0.0), writes=[sl_kTh[1]])
        s.pool(lambda e: e.memset(vp[0][:, :, 64:128], 0.0), writes=[r_vp[0]])
        s.pool(lambda e: e.memset(vp[1][:, :, 0:64], 0.0), writes=[r_vp[1]])
        NR = 3
        psA = [ps(es, nc, f"B_psA{i}", [128, 512], F32) for i in range(NR)]
        psB = [ps(es, nc, f"B_psB{i}", [128, 512], F32) for i in range(NR)]
        psO = [ps(es, nc, f"B_psO{i}", [128, 512], F32) for i in range(2)]
        r_psA = [s.res(f"psA{i}") for i in range(NR)]
        r_psB = [s.res(f"psB{i}") for i in range(NR)]
        r_psO = [s.res(f"psO{i}") for i in range(2)]
        ee = [sb(es, nc, f"B_e{i}", [128, 512], F32) for i in range(NR)]
        sp = [sb(es, nc, f"B_sp{i}", [128, 512], BF16) for i in range(NR)]
        ww = [sb(es, nc, f"B_w{i}", [128, 512], BF16) for i in range(NR)]
        r_e = [s.res(f"e{i}") for i in range(NR)]
        r_sp = [s.res(f"sp{i}") for i in range(NR)]
        r_w = [s.res(f"w{i}") for i in range(NR)]
        ssum = [sb(es, nc, f"B_ssum{h}", [128, 512], F32) for h in range(2)]
        ssbf = [[sb(es, nc, f"B_ssbf{h}{i}", [128, 512], BF16) for i in range(2)] for h in range(2)]
        r_ssum = [s.res(f"ssum{h}") for h in range(2)]
        r_ssbf = [[s.res(f"ssbf{h}{i}") for i in range(2)] for h in range(2)]
        rr = 0
        for hp in range(4):
            s.dma("sp", lambda e, hp=hp: e.dma_start(out=qT[:], in_=qT_d[hp, :, :]), sl_qT, writes=[sl_qT])
            for h in range(2):
                s.dma("sp", lambda e, hp=hp, h=h: e.dma_start(out=kTh[h][64 * h:64 * h + 64, :],
                                                              in_=kT_d[hp, 64 * h:64 * h + 64, :]), sl_kTh[h],
                      writes=[sl_kTh[h]])
            s.dma("sp", lambda e, hp=hp: e.dma_start(
                out=vv[:], in_=v_d[:, hp * 128:(hp + 1) * 128].rearrange("(b p) f -> p b f", p=128)), sl_v,
                writes=[sl_v])
            for h in range(2):
                s.pool(lambda e, h=h: e.tensor_copy(out=vp[h][:, :, 64 * h:64 * h + 64], in_=vv[:, :, 64 * h:64 * h + 64]),
                       reads=[sl_v], writes=[r_vp[h]])
            items = []
            for qt in range(NQ):
                cnt = [0, 0]
                for kb in range(4 * qt + 3, -1, -1):
                    r = kb - 4 * qt
                    for h in range(2):
                        it = dict(qt=qt, kb=kb, h=h, r=r, q0=128 * max(r, 0), diag=(r >= 0), c=cnt[h], i=rr % NR,
                                  po=qt % 2, last=(kb == 0 and h == 1))
                        it["N"] = 512 - it["q0"]
                        rr += 1
                        cnt[h] += 1
                        items.append(it)

            def s1(it):
                i, N, q0, h, kb, qt = it["i"], it["N"], it["q0"], it["h"], it["kb"], it["qt"]
                kk = kTh[h][:, kb * 128:(kb + 1) * 128]
                qq = qT[:, qt * 512 + q0:qt * 512 + 512]
                it["kk"], it["qq"] = kk, qq
                s.pe(lambda e: e.matmul(psA[i][:, 0:N], lhsT=kk, rhs=qq, start=True, stop=True),
                     reads=[sl_qT, sl_kTh[h]], writes=[r_psA[i]])

            def s2a(it):
                i, N = it["i"], it["N"]
                s.act(lambda e: e.activation(out=ee[i][:, 0:N], in_=psA[i][:, 0:N], func=AF.Exp),
                      reads=[r_psA[i]], writes=[r_e[i]])

            def s2b(it):
                i, N, q0, h, c = it["i"], it["N"], it["q0"], it["h"], it["c"]
                s.act(lambda e: e.activation(out=sp[i][:, 0:N], in_=ee[i][:, 0:N], func=AF.Ln, bias=1.0),
                      reads=[r_e[i]], writes=[r_sp[i]])
                if it["diag"]:
                    s.dve(lambda e: e.tensor_tensor(out=sp[i][:, 0:128], in0=sp[i][:, 0:128], in1=mask[:], op=ALU.mult),
                          reads=[r_sp[i], r_const], writes=[r_sp[i]])
                if it["kb"] > 0:
                    if c == 0:
                        s.dve(lambda e: e.memset(ssum[h][:], 0.0), writes=[r_ssum[h]])
                    s.dve(lambda e: e.tensor_tensor(out=ssum[h][:, q0:512], in0=ssum[h][:, q0:512], in1=sp[i][:, 0:N],
                                                    op=ALU.add), reads=[r_sp[i], r_ssum[h]], writes=[r_ssum[h]])
                    s.dve(lambda e: e.tensor_copy(out=ssbf[h][c % 2][:], in_=ssum[h][:]), reads=[r_ssum[h]],
                          writes=[r_ssbf[h][c % 2]])

            def s3(it):
                i, N, q0, h, c = it["i"], it["N"], it["q0"], it["h"], it["c"]
                kk, qq = it["kk"], it["qq"]
                s.pe(lambda e: e.matmul(psB[i][:, 0:N], lhsT=kk, rhs=qq, start=True, stop=False),
                     reads=[sl_qT, sl_kTh[h]], writes=[r_psB[i]])
                s.pe(lambda e: e.matmul(psB[i][:, 0:N], lhsT=uneg[:], rhs=sp[i][:, 0:N], start=False, stop=(c == 0)),
                     reads=[r_sp[i], r_const], writes=[r_psB[i]])
                if c > 0:
                    sbf = ssbf[h][(c - 1) % 2]
                    s.pe(lambda e: e.matmul(psB[i][:, 0:N], lhsT=oneg[:], rhs=sbf[:, q0:512], start=False, stop=True),
                         reads=[r_ssbf[h][(c - 1) % 2], r_const], writes=[r_psB[i]])

            def s4(it):
                i, N = it["i"], it["N"]
                s.act(lambda e: e.activation(out=ww[i][:, 0:N], in_=psB[i][:, 0:N], func=AF.Exp),
                      reads=[r_psB[i]], writes=[r_w[i]])
                if it["diag"]:
                    s.dve(lambda e: e.tensor_tensor(out=ww[i][:, 0:128], in0=ww[i][:, 0:128], in1=mask[:], op=ALU.mult),
                          reads=[r_w[i], r_const], writes=[r_w[i]])

            def s5(it):
                i, N, q0, kb, h, po, qt = it["i"], it["N"], it["q0"], it["kb"], it["h"], it["po"], it["qt"]
                s.pe(lambda e: e.matmul(psO[po][:, q0:512], lhsT=vp[h][:, kb, :], rhs=ww[i][:, 0:N],
                                        start=(it["c"] == 0 and h == 0), stop=(kb == 0 and h == 1),
                                        skip_group_check=True),
                     reads=[r_w[i], r_vp[h]], writes=[r_psO[po]])
                if it["last"]:
                    s.act(lambda e: e.copy(out=oT[:, qt * 512:(qt + 1) * 512], in_=psO[po][:]), reads=[r_psO[po]],
                          writes=[sl_oT])

            n = len(items)
            for t in range(n + 4):
                if t < n:
                    s1(items[t])
                if 0 <= t - 2 < n:
                    s3(items[t - 2])
                if 0 <= t - 4 < n:
                    s5(items[t - 4])
                if 0 <= t - 1 < n:
                    s2a(items[t - 1])
                if 0 <= t - 3 < n:
                    s4(items[t - 3])
                if 0 <= t - 1 < n:
                    s2b(items[t - 1])
            s.dma("sp", lambda e, hp=hp: e.dma_start(out=mixT_d[hp, :, :], in_=oT[:]), sl_oT, reads=[sl_oT])
        s.emit()


def phase_P(nc, sem_pool, S, x_d, mixT_d, w_out, xmid_d, tag):
    NQ = S // 512
    with ExitStack() as es:
        s = Sched(nc, sem_pool)
        W = sb(es, nc, tag + "W", [128, 8, 1024], BF16)
        r_W = s.res("W")
        wslot = s.slot("wslot")
        load_w_bf16(s, W, w_out, 1024, wslot, r_W)
        mt = [sb(es, nc, f"{tag}mt{i}", [128, 8, 512], BF16) for i in range(2)]
        sl_mt = [s.slot(f"mt{i}") for i in range(2)]
        xt = [sb(es, nc, f"{tag}xt{i}", [128, D], F32) for i in range(3)]
        sl_x = [s.slot(f"x{i}") for i in range(3)]
        psY = [ps(es, nc, f"{tag}psY{i}", [128, 512], F32) for i in range(4)]
        r_psY = [s.res(f"psY{i}") for i in range(4)]
        k = 0
        for qt in range(NQ):
            b = qt % 2
            s.dma("sp", lambda e, qt=qt, b=b: e.dma_start(
                out=mt[b][:], in_=mixT_d[:, :, qt * 512:(qt + 1) * 512].rearrange("h p t -> p h t")), sl_mt[b],
                writes=[sl_mt[b]])
            for st in range(4):
                t0 = qt * 512 + st * 128
                xb_ = k % 3
                pb = (k % 2) * 2
                k += 1
                s.dma("sp", lambda e, t0=t0, xb_=xb_: e.dma_start(out=xt[xb_][:], in_=x_d[t0:t0 + 128, :]), sl_x[xb_],
                      writes=[sl_x[xb_]])
                for n in range(2):
                    for c in range(8):
                        s.pe(lambda e, n=n, c=c, b=b, st=st, pb=pb: e.matmul(
                            psY[pb + n][:], lhsT=mt[b][:, c, st * 128:(st + 1) * 128], rhs=W[:, c, n * 512:(n + 1) * 512],
                            start=(c == 0), stop=(c == 7)), reads=[sl_mt[b], r_W], writes=[r_psY[pb + n]])
                    s.dve(lambda e, n=n, xb_=xb_, pb=pb: e.tensor_tensor(
                        out=xt[xb_][:, n * 512:(n + 1) * 512], in0=xt[xb_][:, n * 512:(n + 1) * 512], in1=psY[pb + n][:],
                        op=ALU.add), reads=[r_psY[pb + n], sl_x[xb_]], writes=[sl_x[xb_]])
                s.dma("sp", lambda e, t0=t0, xb_=xb_: e.dma_start(out=xmid_d[t0:t0 + 128, :], in_=xt[xb_][:]), sl_x[xb_],
                      reads=[sl_x[xb_]])
        s.emit()


def phase_F(nc, sem_pool, xm_d, out_d, qtiles, gain_d, w_up, conv_w, conv_b, w_down, cst, tag):
    NJ = FH // 128
    with ExitStack() as es:
        s = Sched(nc, sem_pool)
        ident = sb(es, nc, tag + "ident", [128, 128], BF16)
        r_const = s.res("const")
        cslot = s.slot("cslot")
        s.dma("pool", lambda e: e.dma_start(out=ident[:], in_=cst["ident"][:, :]), cslot, writes=[r_const])
        WU = sb(es, nc, tag + "WU", [128, 8, 2 * FH], BF16)
        r_WU = s.res("WU")
        wslot = s.slot("wslot")
        load_w_bf16(s, WU, w_up, 1024, wslot, r_WU)
        cw = sb(es, nc, tag + "cw", [128, 3, 2 * NJ], F32)
        cb = sb(es, nc, tag + "cb", [128, 2 * NJ], F32)
        s.dma("pool", lambda e: e.dma_start(out=cw[:], in_=conv_w[:, :, :]), cslot, writes=[r_const])
        s.dma("pool", lambda e: e.dma_start(out=cb[:], in_=conv_b[:, :]), cslot, writes=[r_const])
        halo = sb(es, nc, tag + "halo", [128, 2 * NJ, 2], F32)
        r_halo = [s.res(f"halo{j}") for j in range(2 * NJ)]
        s.dve(lambda e: e.memset(halo[:], 0.0), writes=r_halo)
        nrm = NormCtx(es, nc, s, tag + "n", gain_d, ident)
        xt = [sb(es, nc, f"{tag}xt{i}", [128, D], F32) for i in range(4)]
        sl_x = [s.slot(f"x{i}") for i in range(4)]
        hT = [sb(es, nc, f"{tag}hT{i}", [128, 8, 512], BF16) for i in range(2)]
        r_hT = [s.res(f"hT{i}") for i in range(2)]
        mm = sb(es, nc, tag + "m", [128, NJ, 512], BF16)
        r_m = s.res("m")
        NWD = 8
        wd = [sb(es, nc, f"{tag}wd{i}", [128, 512], BF16) for i in range(NWD)]
        sl_wd = [s.slot(f"wd{i}") for i in range(NWD)]
        ub = [[sb(es, nc, f"{tag}ub{h}{i}", [128, 514], F32) for i in range(2)] for h in range(2)]
        r_ub = [[s.res(f"ub{h}{i}") for i in range(2)] for h in range(2)]
        cc = [[sb(es, nc, f"{tag}cc{h}{i}", [128, 512], F32) for i in range(2)] for h in range(2)]
        r_cc = [[s.res(f"cc{h}{i}") for i in range(2)] for h in range(2)]
        bank = [ps(es, nc, f"{tag}bank{i}", [128, 512], F32) for i in range(4)]
        r_bank = [s.res(f"bank{i}") for i in range(4)]
        psTs = [ps(es, nc, f"{tag}psT{i}", [128, 8, 128], BF16) for i in range(2)]
        r_psTs = [s.res(f"psT{i}") for i in range(2)]
        state = {"wdk": 0, "nk": 0}

        def norm_p1(ti, st):
            row0, nsub, _ = qtiles[ti]
            k = state["nk"]
            state["nk"] += 1
            xs = k % 4
            s.dma("sp", lambda e: e.dma_start(out=xt[xs][:], in_=xm_d[row0 + st * 128:row0 + (st + 1) * 128, :]),
                  sl_x[xs], writes=[sl_x[xs]])
            state["nn"] = state.get("nn", 0) + 1
            kk = state["nn"]
            nrm.part1(kk, xt[xs][:], sl_x[xs])
            return kk

        def norm_p2(ti, st, kk):
            hb_ = ti % 2
            nrm.part2(kk, psTs[kk % 2], r_psTs[kk % 2], hT[hb_][:, :, st * 128:(st + 1) * 128], r_hT[hb_])

        def norm_sub(ti, st):
            norm_p2(ti, st, norm_p1(ti, st))

        def wd_load(n, j):
            w = state["wdk"] % NWD
            state["wdk"] += 1
            s.dma("pool", lambda e: e.dma_start(out=wd[w][:], in_=w_down[j * 128:(j + 1) * 128, n * 512:(n + 1) * 512]),
                  sl_wd[w], writes=[sl_wd[w]])
            return w

        for st in range(qtiles[0][1]):
            norm_sub(0, st)
        for ti, (row0, nsub, dst0) in enumerate(qtiles):
            TW = nsub * 128
            hb_ = ti % 2
            hTt = hT[hb_]
            nxt = []
            pend = None
            if ti + 1 < len(qtiles):
                nxt = [(ti + 1, st) for st in range(qtiles[ti + 1][1])]
            pre_w = []
            for j in range(NJ):
                db = j % 2
                for half in range(2):
                    jj = half * NJ + j
                    pb = 2 * db + half
                    for c in range(8):
                        s.pe(lambda e, c=c, jj=jj, pb=pb, TW=TW, hTt=hTt: e.matmul(bank[pb][:, 0:TW], lhsT=WU[:, c, jj * 128:(jj + 1) * 128],
                                                                  rhs=hTt[:, c, 0:TW], start=(c == 0), stop=(c == 7)),
                             reads=[r_WU, r_hT[hb_]], writes=[r_bank[pb]])
                    u = ub[half][db]
                    o = cc[half][db]
                    ru, ro = r_ub[half][db], r_cc[half][db]
                    s.act(lambda e, u=u, pb=pb, TW=TW: e.copy(out=u[:, 2:2 + TW], in_=bank[pb][:, 0:TW]), reads=[r_bank[pb]],
                          writes=[ru])
                    s.act(lambda e, o=o, pb=pb, jj=jj, TW=TW: e.activation(out=o[:, 0:TW], in_=bank[pb][:, 0:TW], func=AF.Identity,
                                                                    scale=cw[:, 2, jj:jj + 1], bias=cb[:, jj:jj + 1]),
                          reads=[r_bank[pb], r_const], writes=[ro])
                    s.pool(lambda e, u=u, jj=jj: e.tensor_copy(out=u[:, 0:2], in_=halo[:, jj, :]), reads=[r_halo[jj]],
                           writes=[ru])
                    s.dve(lambda e, u=u, jj=jj, TW=TW: e.tensor_copy(out=halo[:, jj, :], in_=u[:, TW:TW + 2]), reads=[ru],
                          writes=[r_halo[jj]])
                    s.dve(lambda e, u=u, o=o, jj=jj, TW=TW: e.scalar_tensor_tensor(out=o[:, 0:TW], in0=u[:, 1:1 + TW],
                                                                            scalar=cw[:, 1, jj:jj + 1], in1=o[:, 0:TW],
                                                                            op0=ALU.mult, op1=ALU.add),
                          reads=[ru, r_const, ro], writes=[ro])
                for half in range(2):
                    jj = half * NJ + j
                    u = ub[half][db]
                    o = cc[half][db]
                    ru, ro = r_ub[half][db], r_cc[half][db]
                    s.dve(lambda e, u=u, o=o, jj=jj, TW=TW: e.scalar_tensor_tensor(out=o[:, 0:TW], in0=u[:, 0:TW],
                                                                            scalar=cw[:, 0, jj:jj + 1], in1=o[:, 0:TW],
                                                                            op0=ALU.mult, op1=ALU.add),
                          reads=[ru, r_const, ro], writes=[ro])
                oa, og_ = cc[0][db], cc[1][db]
                s.act(lambda e, oa=oa, j=j, TW=TW: e.activation(out=oa[:, 0:TW], in_=oa[:, 0:TW], func=AF.Silu),
                      reads=[r_cc[0][db], r_const], writes=[r_cc[0][db]])
                s.pool(lambda e, oa=oa, og_=og_, j=j, TW=TW: e.tensor_tensor(out=mm[:, j, 0:TW], in0=oa[:, 0:TW], in1=og_[:, 0:TW],
                                                                      op=ALU.mult),
                       reads=[r_cc[0][db], r_cc[1][db], r_const], writes=[r_m])
                if nxt and j in (1, 6, 11, 16):
                    pend = (nxt[0], norm_p1(*nxt.pop(0)))
                if j in (4, 9, 14, 19) and pend is not None:
                    norm_p2(pend[0][0], pend[0][1], pend[1])
                    pend = None
                if j >= NJ - NWD:
                    pre_w.append(wd_load(0, len(pre_w)))
            if pend is not None:
                norm_p2(pend[0][0], pend[0][1], pend[1])
                pend = None
            while nxt:
                norm_sub(*nxt.pop(0))
            if dst0 is None:
                continue
            for n in range(2):
                for j in range(NJ):
                    if n == 0 and j < len(pre_w):
                        w = pre_w[j]
                    else:
                        w = wd_load(n, j)
                    for st in range(nsub):
                        s.pe(lambda e, j=j, st=st, w=w: e.matmul(bank[st][:], lhsT=mm[:, j, st * 128:(st + 1) * 128],
                                                                  rhs=wd[w][:], start=(j == 0), stop=(j == NJ - 1)),
                             reads=[r_m, sl_wd[w]], writes=[r_bank[st]])
                if n == 0:
                    xs_of = []
                    for st in range(nsub):
                        k = state["nk"]
                        state["nk"] += 1
                        xs = k % 4
                        xs_of.append(xs)
                        s.dma("sp", lambda e, xs=xs, st=st, row0=row0: e.dma_start(
                            out=xt[xs][:], in_=xm_d[row0 + st * 128:row0 + (st + 1) * 128, :]), sl_x[xs],
                            writes=[sl_x[xs]])
                for st in range(nsub):
                    xs = xs_of[st]
                    s.dve(lambda e, st=st, n=n, xs=xs: e.tensor_tensor(out=xt[xs][:, n * 512:(n + 1) * 512],
                                                                      in0=xt[xs][:, n * 512:(n + 1) * 512],
                                                                      in1=bank[st][:], op=ALU.add),
                          reads=[r_bank[st], sl_x[xs]], writes=[sl_x[xs]])
            for st in range(nsub):
                xs = xs_of[st]
                s.dma("sp", lambda e, xs=xs, st=st, dst0=dst0: e.dma_start(out=out_d[dst0 + st * 128:dst0 + (st + 1) * 128, :],
                                                               in_=xt[xs][:]), sl_x[xs], reads=[sl_x[xs]])
        s.emit()


def phase_DE(nc, sem_pool, S, x_d, xmid_d, prm, cst):
    NT = S // 128
    CQ, CK, CV, CR, CG = 0, 512, 1024, 2048, 3072
    with ExitStack() as es:
        s = Sched(nc, sem_pool)
        ident = sb(es, nc, "E_ident", [128, 128], BF16)
        m1 = sb(es, nc, "E_m1", [128, 128], F32)
        msel = sb(es, nc, "E_msel", [128, 2], F32)
        og = sb(es, nc, "E_og", [128, D], F32)
        wa2b = sb(es, nc, "E_wa2b", [17, 512], BF16)
        r_const = s.res("const")
        cslot = s.slot("cslot")
        s.dma("pool", lambda e: e.dma_start(out=ident[:], in_=cst["ident"][:, :]), cslot, writes=[r_const])
        s.dma("pool", lambda e: e.dma_start(out=m1[:], in_=cst["m1"][:, :]), cslot, writes=[r_const])
        s.dma("pool", lambda e: e.dma_start(out=msel[:], in_=cst["msel"][:, :]), cslot, writes=[r_const])
        s.dma("pool", lambda e: e.dma_start(out=og[:], in_=prm["og_t"].broadcast_to([128, D])), cslot, writes=[r_const])
        s.dma("pool", lambda e: e.dma_start(out=wa2b[0:16, :], in_=prm["gla_w_a2"][:, :]), cslot, writes=[r_const])
        s.dma("pool", lambda e: e.dma_start(out=wa2b[16:17, :], in_=prm["gla_b_a"][:, :]), cslot, writes=[r_const])
        W = sb(es, nc, "E_W", [128, 8, 3088], BF16)
        WO = sb(es, nc, "E_WO", [128, 8, 1024], BF16)
        r_W, r_WO = s.res("W"), s.res("WO")
        wslot, woslot = s.slot("wslot"), s.slot("woslot")
        load_w_bf16(s, W, prm["w_in_odd"], 1024, wslot, r_W)
        load_w_bf16(s, WO, prm["w_out_odd"], 1024, woslot, r_WO)
        nrm = NormCtx(es, nc, s, "E_n", prm["mix_norm_odd"], ident)
        xt = [sb(es, nc, f"E_xt{i}", [128, D], F32) for i in range(3)]
        sl_x = [s.slot(f"x{i}") for i in range(3)]
        hT = sb(es, nc, "E_hT", [128, 8, 128], BF16)
        r_hT = s.res("hT")
        bank = [None] + [ps(es, nc, f"E_bank{i}", [128, 512], F32) for i in range(1, 8)]
        r_bank = [s.res(f"bank{i}") for i in range(8)]
        psT0 = ps(es, nc, "E_psT0", [128, 8, 128], BF16)
        al = sb(es, nc, "E_al", [17, 128], BF16)
        r_al = s.res("al")
        s.dve(lambda e: e.memset(al[:], 1.0), writes=[r_al])
        kf = sb(es, nc, "E_kf", [128, 512], F32)
        vb = sb(es, nc, "E_vb", [128, 1024], BF16)
        sr = sb(es, nc, "E_sr", [128, 1024], F32)
        qTs = sb(es, nc, "E_qTs", [128, 4, 128], BF16)
        eg = sb(es, nc, "E_eg", [128, 512], F32)
        spg = sb(es, nc, "E_spg", [128, 512], F32)
        ed = sb(es, nc, "E_ed", [128, 512], F32)
        kdec = sb(es, nc, "E_kdec", [128, 512], BF16)
        dec = sb(es, nc, "E_dec", [128, 4, 2], F32)
        state = sb(es, nc, "E_state", [128, 4, 256], F32)
        stbf = sb(es, nc, "E_stbf", [128, 4, 256], BF16)
        osb = sb(es, nc, "E_osb", [128, D], F32)
        osq = sb(es, nc, "E_osq", [128, D], F32)
        ss4 = sb(es, nc, "E_ss4", [128, 4], F32)
        ln4 = sb(es, nc, "E_ln4", [128, 4], F32)
        r4 = sb(es, nc, "E_r4", [128, 4], F32)
        gated = [sb(es, nc, f"E_gated{i}", [128, D], BF16) for i in range(2)]
        goT = sb(es, nc, "E_goT", [128, 8, 128], BF16)
        (r_kf, r_vb, r_sr, r_qTs, r_eg, r_spg, r_ed, r_kdec, r_dec, r_osb, r_osq, r_ss4, r_ln4, r_r4, r_gated,
         r_goT) = (s.res(n) for n in ("kf", "vb", "sr", "qTs", "eg", "spg", "ed", "kdec", "dec", "osb", "osq", "ss4",
                                      "ln4", "r4", "gatedx", "goT"))
        r_gated2 = [s.res(f"gated{i}") for i in range(2)]
        r_kv = [s.res(f"kv{h}") for h in range(4)]
        r_state = [s.res(f"state{h}") for h in range(4)]
        r_stbf = [s.res(f"stbf{h}") for h in range(4)]
        s.dve(lambda e: e.memset(state[:], 0.0), writes=r_state)

        def proj_tok(pb, col0):
            for c in range(8):
                s.pe(lambda e, c=c: e.matmul(bank[pb][:], lhsT=hT[:, c, :], rhs=W[:, c, col0:col0 + 512], start=(c == 0),
                                             stop=(c == 7)), reads=[r_hT, r_W], writes=[r_bank[pb]])

        def de_pre(i):
            b = i % 3
            s.dma("sp", lambda e: e.dma_start(out=xt[b][:], in_=x_d[i * 128:(i + 1) * 128, :]), sl_x[b],
                  writes=[sl_x[b]])
            nrm.part1(i, xt[b][:], sl_x[b])

        de_pre(0)
        pending = []
        for i in range(NT):
            b = i % 3
            nrm.part2(i, psT0, r_bank[0], hT[:], r_hT)
            proj_tok(1, CK)
            proj_tok(2, CV)
            proj_tok(3, CV + 512)
            while pending:
                pending.pop(0)()
            proj_tok(4, CR)
            proj_tok(5, CR + 512)
            if i + 1 < NT:
                de_pre(i + 1)
            for h in range(4):
                for c in range(8):
                    s.pe(lambda e, h=h, c=c: e.matmul(bank[6][:, h * 128:(h + 1) * 128],
                                                      lhsT=W[:, c, CQ + h * 128:CQ + (h + 1) * 128], rhs=hT[:, c, :],
                                                      start=(c == 0), stop=(c == 7)),
                         reads=[r_hT, r_W], writes=[r_bank[6]])
            for c in range(8):
                s.pe(lambda e, c=c: e.matmul(bank[7][0:16, 0:128], lhsT=W[:, c, CG:CG + 16], rhs=hT[:, c, :],
                                             start=(c == 0), stop=(c == 7)), reads=[r_hT, r_W], writes=[r_bank[7]])
            s.act(lambda e: e.copy(out=kf[:], in_=bank[1][:]), reads=[r_bank[1]], writes=[r_kf])
            s.dve(lambda e: e.tensor_copy(out=vb[:, 0:512], in_=bank[2][:]), reads=[r_bank[2]], writes=[r_vb])
            s.dve(lambda e: e.tensor_copy(out=vb[:, 512:1024], in_=bank[3][:]), reads=[r_bank[3]], writes=[r_vb])
            s.act(lambda e: e.activation(out=sr[:, 0:512], in_=bank[4][:], func=AF.Silu), reads=[r_bank[4]],
                  writes=[r_sr])
            s.act(lambda e: e.activation(out=sr[:, 512:1024], in_=bank[5][:], func=AF.Silu), reads=[r_bank[5]],
                  writes=[r_sr])
            s.act(lambda e: e.activation(out=qTs[:], in_=bank[6][:].rearrange("p (h t) -> p h t", h=4), func=AF.Copy,
                                         scale=float(128 ** -0.5)), reads=[r_bank[6]], writes=[r_qTs])
            s.dve(lambda e: e.tensor_copy(out=al[0:16, :], in_=bank[7][0:16, 0:128]), reads=[r_bank[7]], writes=[r_al])
            s.pe(lambda e: e.matmul(bank[1][:], lhsT=al[:], rhs=wa2b[:], start=True, stop=True),
                 reads=[r_al, r_const], writes=[r_bank[1]])
            s.act(lambda e: e.activation(out=eg[:], in_=bank[1][:], func=AF.Exp, scale=-1.0), reads=[r_bank[1]],
                  writes=[r_eg])
            s.act(lambda e: e.activation(out=spg[:], in_=eg[:], func=AF.Ln, bias=1.0), reads=[r_eg], writes=[r_spg])
            s.pe(lambda e: e.matmul(bank[7][:], lhsT=m1[:], rhs=spg[:], start=True, stop=True),
                 reads=[r_spg, r_const], writes=[r_bank[7]])
            for h in range(4):
                s.pe(lambda e, h=h: e.matmul(bank[6][:, h * 2:h * 2 + 2], lhsT=spg[:, h * 128:(h + 1) * 128], rhs=msel[:],
                                             start=True, stop=True), reads=[r_spg, r_const], writes=[r_bank[6]])
            s.act(lambda e: e.activation(out=ed[:], in_=bank[7][:], func=AF.Exp), reads=[r_bank[7]], writes=[r_ed])
            s.act(lambda e: e.activation(out=dec[:], in_=bank[6][:, 0:8].rearrange("p (h c) -> p h c", h=4), func=AF.Exp),
                  reads=[r_bank[6]], writes=[r_dec])
            s.dve(lambda e: e.tensor_tensor(out=kdec[:], in0=kf[:], in1=ed[:], op=ALU.mult), reads=[r_kf, r_ed],
                  writes=[r_kdec])
            for c2 in range(2):
                cs = slice(64 * c2, 64 * c2 + 64)
                for h in range(4):
                    pk = 4 + (h // 2)
                    ko = (h % 2) * 256
                    s.pe(lambda e, h=h, cs=cs, pk=pk, ko=ko: e.matmul(bank[pk][:, ko:ko + 256],
                                                                     lhsT=kdec[cs, h * 128:(h + 1) * 128],
                                                                     rhs=vb[cs, h * 256:(h + 1) * 256], start=True,
                                                                     stop=True),
                         reads=[r_kdec, r_vb], writes=[r_kv[h], r_bank[pk]])
                for h in range(4):
                    pk = 4 + (h // 2)
                    ko = (h % 2) * 256
                    s.dve(lambda e, h=h, c2=c2, pk=pk, ko=ko: e.scalar_tensor_tensor(
                        out=state[:, h, :], in0=state[:, h, :], scalar=dec[:, h, c2:c2 + 1], in1=bank[pk][:, ko:ko + 256],
                        op0=ALU.mult, op1=ALU.add), reads=[r_state[h], r_dec, r_kv[h]], writes=[r_state[h]])
                    s.act(lambda e, h=h: e.copy(out=stbf[:, h, :], in_=state[:, h, :]), reads=[r_state[h]],
                          writes=[r_stbf[h]])
                for h in range(4):
                    ko = (h % 2) * 256
                    po = 2 + (h // 2)
                    s.pe(lambda e, h=h, cs=cs, po=po, ko=ko: e.matmul(bank[po][cs, ko:ko + 256], lhsT=qTs[:, h, cs],
                                                                     rhs=stbf[:, h, :], start=True, stop=True),
                         reads=[r_qTs, r_stbf[h]], writes=[r_bank[po]])
            s.act(lambda e: e.copy(out=osb[:, 0:512], in_=bank[2][:]), reads=[r_bank[2]], writes=[r_osb])
            s.act(lambda e: e.copy(out=osb[:, 512:1024], in_=bank[3][:]), reads=[r_bank[3]], writes=[r_osb])
            s.dve(lambda e: e.tensor_tensor(out=osq[:], in0=osb[:], in1=osb[:], op=ALU.mult), reads=[r_osb],
                  writes=[r_osq])
            s.dve(lambda e: e.tensor_reduce(out=ss4[:], in_=osq[:].rearrange("p (a b) -> p a b", a=4), axis=AX.X,
                                            op=ALU.add), reads=[r_osq], writes=[r_ss4])
            rms_rstd(s, ss4[:], ln4[:], r4[:], 256, r_ss4, r_ln4, r_r4)
            for h in range(4):
                hs = slice(h * 256, (h + 1) * 256)
                s.dve(lambda e, h=h, hs=hs: e.scalar_tensor_tensor(out=osb[:, hs], in0=osb[:, hs], scalar=r4[:, h:h + 1],
                                                                   in1=og[:, hs], op0=ALU.mult, op1=ALU.mult),
                      reads=[r_osb, r_r4, r_const], writes=[r_osb])
            gi = i % 2
            s.dve(lambda e, gi=gi: e.tensor_tensor(out=gated[gi][:], in0=osb[:], in1=sr[:], op=ALU.mult),
                  reads=[r_osb, r_sr], writes=[r_gated2[gi]])

            def deferred(i=i, b=b, gi=gi):
                for c in range(8):
                    s.pe(lambda e, c=c: e.transpose(out=psT0[:, c, :], in_=gated[gi][:, c * 128:(c + 1) * 128],
                                                    identity=ident[:]), reads=[r_gated2[gi], r_const],
                         writes=[r_bank[0]])
                s.act(lambda e: e.copy(out=goT[:], in_=psT0[:]), reads=[r_bank[0]], writes=[r_goT])
                for n in range(2):
                    for c in range(8):
                        s.pe(lambda e, n=n, c=c: e.matmul(bank[4 + n][:], lhsT=goT[:, c, :],
                                                          rhs=WO[:, c, n * 512:(n + 1) * 512], start=(c == 0), stop=(c == 7)),
                             reads=[r_goT, r_WO], writes=[r_bank[4 + n]])
                    s.dve(lambda e, n=n: e.tensor_tensor(out=xt[b][:, n * 512:(n + 1) * 512],
                                                         in0=xt[b][:, n * 512:(n + 1) * 512], in1=bank[4 + n][:],
                                                         op=ALU.add), reads=[r_bank[4 + n], sl_x[b]], writes=[sl_x[b]])
                s.dma("sp", lambda e: e.dma_start(out=xmid_d[i * 128:(i + 1) * 128, :], in_=xt[b][:]), sl_x[b],
                      reads=[sl_x[b]])

            pending.append(deferred)
        while pending:
            pending.pop(0)()
        s.emit()


def phase_S(nc, sem_pool, S, xin_d, xsel_d, flags_d):
    H = S // 2
    with ExitStack() as es:
        s = Sched(nc, sem_pool)
        fl = sb(es, nc, "S_fl", [128, 2], F32)
        r_const = s.res("const")
        cslot = s.slot("cslot")
        s.dma("pool", lambda e: e.dma_start(out=fl[:], in_=flags_d[:, :]), cslot, writes=[r_const])
        NB_ = 3
        xa = [sb(es, nc, f"S_xa{i}", [128, D], F32) for i in range(NB_)]
        xb_ = [sb(es, nc, f"S_xb{i}", [128, D], F32) for i in range(NB_)]
        sl_a = [s.slot(f"a{i}") for i in range(NB_)]
        sl_b = [s.slot(f"b{i}") for i in range(NB_)]
        s.dma("sp", lambda e: e.dma_start(out=xa[0][:], in_=xin_d[H - 128:H, :]), sl_a[0], writes=[sl_a[0]])
        s.act(lambda e: e.activation(out=xa[0][:], in_=xa[0][:], func=AF.Copy, scale=fl[:, 1:2]),
              reads=[sl_a[0], r_const], writes=[sl_a[0]])
        s.dma("sp", lambda e: e.dma_start(out=xsel_d[0:128, :], in_=xa[0][:]), sl_a[0], reads=[sl_a[0]])
        for i in range(H // 128):
            k = (i + 1) % NB_
            s.dma("sp", lambda e, i=i, k=k: e.dma_start(out=xa[k][:], in_=xin_d[i * 128:(i + 1) * 128, :]), sl_a[k],
                  writes=[sl_a[k]])
            s.dma("sp", lambda e, i=i, k=k: e.dma_start(out=xb_[k][:], in_=xin_d[H + i * 128:H + (i + 1) * 128, :]),
                  sl_b[k], writes=[sl_b[k]])
            s.act(lambda e, k=k: e.activation(out=xa[k][:], in_=xa[k][:], func=AF.Copy, scale=fl[:, 0:1]),
                  reads=[sl_a[k], r_const], writes=[sl_a[k]])
            s.dve(lambda e, k=k: e.scalar_tensor_tensor(out=xa[k][:], in0=xb_[k][:], scalar=fl[:, 1:2], in1=xa[k][:],
                                                        op0=ALU.mult, op1=ALU.add),
                  reads=[sl_a[k], sl_b[k], r_const], writes=[sl_a[k]])
            s.dma("sp", lambda e, i=i, k=k: e.dma_start(out=xsel_d[128 + i * 128:128 + (i + 1) * 128, :], in_=xa[k][:]),
                  sl_a[k], reads=[sl_a[k]])
        s.emit()


PARAM_SHAPES = {
    "mix_norm_even": [1, D], "w_in_even": [D, 2048], "qg_t": [1, 512], "kg_t": [1, 512],
    "pool_w": [4, 128, 128], "pool_scale": [128, 4], "w_out_even": [D, D],
    "mix_norm_odd": [1, D], "w_in_odd": [D, 3088], "gla_w_a2": [16, 512], "gla_b_a": [1, 512],
    "og_t": [1, D], "w_out_odd": [D, D],
    "ffn_norm0": [1, D], "ffn_norm1": [1, D], "ffn_w_up0": [D, 2 * FH], "ffn_w_up1": [D, 2 * FH],
    "ffn_conv_w0": [128, 3, 44], "ffn_conv_w1": [128, 3, 44], "ffn_conv_b0": [128, 44], "ffn_conv_b1": [128, 44],
    "ffn_w_down0": [FH, D], "ffn_w_down1": [FH, D],
}
CONST_SHAPES = {
    "ident": ([128, 128], np.float32), "mask": ([128, 128], np.float32), "uneg": ([128, 128], np.float32),
    "oneg": ([128, 128], np.float32), "poolT": ([128, 12, 128], np.float32), "m1": ([128, 128], np.float32),
    "msel": ([128, 2], np.float32),
}


def make_consts():
    c = {}
    j = np.arange(128)
    c["ident"] = np.eye(128, dtype=np.float32)
    c["mask"] = (j[:, None] < j[None, :]).astype(np.float32)
    c["uneg"] = -(j[:, None] >= j[None, :]).astype(np.float32)
    c["oneg"] = -np.ones((128, 128), np.float32)
    pt = np.zeros((128, 12, 128), np.float32)
    for g, w in enumerate((2, 4, 8, 16)):
        s_ = j[:, None]
        t_ = j[None, :]
        band = ((s_ <= t_) & (s_ > t_ - w)).astype(np.float32)
        pt[:, g, :] = band / w - np.eye(128, dtype=np.float32)
        prev = ((s_ - 128 <= t_) & (s_ - 128 > t_ - w)).astype(np.float32)
        pt[:, 4 + g, :] = prev / w
        cnt = np.minimum(j + 1, w).astype(np.float32)[None, :]
        pt[:, 8 + g, :] = band / cnt - np.eye(128, dtype=np.float32)
    c["poolT"] = pt
    same = (j[:, None] // 64) == (j[None, :] // 64)
    c["m1"] = (-(1.0 / 16.0) * ((j[:, None] > j[None, :]) & same)).astype(np.float32)
    ms = np.zeros((128, 2), np.float32)
    ms[:64, 0] = -1.0 / 16.0
    ms[64:, 1] = -1.0 / 16.0
    c["msel"] = ms
    return c


def build_nc(S, debug=False, n_phases=6):
    nc = bass.Bass("TRN2", target_bir_lowering=False)
    x_d = nc.dram_tensor("x", [S, D], F32, kind="ExternalInput").ap()
    prm = {k: nc.dram_tensor(k, v, F32, kind="ExternalInput").ap() for k, v in PARAM_SHAPES.items()}
    cst = {k: nc.dram_tensor("c_" + k, v[0], F32, kind="ExternalInput").ap() for k, v in CONST_SHAPES.items()}
    out_d = nc.dram_tensor("out", [S // 2, D], F32, kind="ExternalOutput").ap()
    flags_d = nc.dram_tensor("flags", [128, 2], F32, kind="ExternalInput").ap()
    kind = "ExternalOutput" if debug else "Internal"
    qT_d = nc.dram_tensor("qT_d", [4, 128, S], BF16, kind=kind).ap()
    kT_d = nc.dram_tensor("kT_d", [4, 128, S], BF16, kind=kind).ap()
    v_d = nc.dram_tensor("v_d", [S, 512], BF16, kind=kind).ap()
    mixT_d = nc.dram_tensor("mixT_d", [8, 128, S], BF16, kind=kind).ap()
    xmid_d = nc.dram_tensor("xmid_d", [S, D], F32, kind=kind).ap()
    x1_d = nc.dram_tensor("x1_d", [S, D], F32, kind=kind).ap()
    xmid1_d = nc.dram_tensor("xmid1_d", [S, D], F32, kind=kind).ap()
    xsel_d = nc.dram_tensor("xsel_d", [128 + S // 2, D], F32, kind="Internal").ap()
    with ExitStack() as ges:
        sem_pool = SemPool(nc, ges)
        if n_phases >= 1:
            phase_A(nc, sem_pool, S, x_d, prm, cst, qT_d, kT_d, v_d, mixT_d)
        if n_phases >= 2:
            phase_B(nc, sem_pool, S, cst, qT_d, kT_d, v_d, mixT_d)
        if n_phases >= 3:
            phase_P(nc, sem_pool, S, x_d, mixT_d, prm["w_out_even"], xmid_d, "P0")
        if n_phases >= 4:
            qt_all = [(q * 512, 4, q * 512) for q in range(S // 512)]
            phase_F(nc, sem_pool, xmid_d, x1_d, qt_all, prm["ffn_norm0"], prm["ffn_w_up0"], prm["ffn_conv_w0"],
                    prm["ffn_conv_b0"], prm["ffn_w_down0"], cst, "F0")
        if n_phases >= 5:
            phase_DE(nc, sem_pool, S, x1_d, xmid1_d, prm, cst)
        if n_phases >= 6:
            phase_S(nc, sem_pool, S, xmid1_d, xsel_d, flags_d)
            qt_half = [(0, 1, None)] + [(128 + q * 512, 4, q * 512) for q in range(S // 1024)]
            phase_F(nc, sem_pool, xsel_d, out_d, qt_half, prm["ffn_norm1"], prm["ffn_w_up1"], prm["ffn_conv_w1"],
                    prm["ffn_conv_b1"], prm["ffn_w_down1"], cst, "F1")
    return nc


def make_param_maps(inp):
    f = lambda a: np.ascontiguousarray(np.asarray(a, dtype=np.float32))
    p = {
        "mix_norm_even": f(inp["mix_norm_even"][0][None, :]), "w_in_even": f(inp["w_in_even"][0]),
        "qg_t": f(np.tile(np.asarray(inp["sb_q_gain"][0]), 8)[None, :]),
        "kg_t": f(np.tile(np.asarray(inp["sb_k_gain"][0]), 8)[None, :]),
        "pool_w": f(inp["pool_w"][0]), "pool_scale": f(np.asarray(inp["pool_scale"][0]).reshape(4, 128).T), "w_out_even": f(inp["w_out_even"][0]),
        "mix_norm_odd": f(inp["mix_norm_odd"][0][None, :]), "w_in_odd": f(inp["w_in_odd"][0]),
        "gla_w_a2": f(inp["gla_w_a2"][0]), "gla_b_a": f(inp["gla_b_a"][0][None, :]),
        "og_t": f(np.tile(np.asarray(inp["gla_o_gain"][0]), 4)[None, :]), "w_out_odd": f(inp["w_out_odd"][0]),
    }
    for l in range(2):
        p[f"ffn_norm{l}"] = f(inp["ffn_norm"][l][None, :])
        p[f"ffn_w_up{l}"] = f(inp["ffn_w_up"][l])
        p[f"ffn_conv_w{l}"] = f(np.asarray(inp["ffn_conv_w"][l]).reshape(3, 44, 128).transpose(2, 0, 1))
        p[f"ffn_conv_b{l}"] = f(np.asarray(inp["ffn_conv_b"][l]).reshape(44, 128).T)
        p[f"ffn_w_down{l}"] = f(inp["ffn_w_down"][l])
    for k, v in make_consts().items():
        p["c_" + k] = v
    return p


def kernel(**inputs):
    x = np.asarray(inputs["x"], dtype=np.float32)
    B, S, _ = x.shape
    nc = build_nc(S)
    pm = make_param_maps(inputs)
    n_cores = 8
    in_maps = []
    for c in range(n_cores):
        m = dict(pm)
        m["x"] = np.ascontiguousarray(x[(c // 2) % B])
        fl = np.zeros((128, 2), np.float32)
        fl[:, c % 2] = 1.0
        m["flags"] = fl
        in_maps.append(m)
    res = run_bass_kernel_spmd(nc, in_maps, core_ids=list(range(n_cores)))
    out = np.stack([np.concatenate([np.asarray(res.results[2 * b]["out"], dtype=np.float32),
                                    np.asarray(res.results[2 * b + 1]["out"], dtype=np.float32)], axis=0)
                    for b in range(B)], axis=0)
    return out
```

```python
from contextlib import ExitStack
import numpy as np
import ml_dtypes
import concourse.bass as bass
import concourse.mybir as mybir
from concourse.bass_utils import run_bass_kernel_spmd

F32 = mybir.dt.float32
BF16 = mybir.dt.bfloat16
AF = mybir.ActivationFunctionType
ALU = mybir.AluOpType
AX = mybir.AxisListType

D = 1024
FH = 2816
EPS = 1e-6
SAME_ENGINE_SYNC = True


class Res:
    __slots__ = ("name", "w", "rs", "sem", "dcnt")

    def __init__(self, name):
        self.name = name
        self.w = None
        self.rs = []
        self.sem = None
        self.dcnt = 0


class Eng:
    def __init__(self, name, sem):
        self.name = name
        self.sem = sem
        self.count = 0
        self.seen = {}
        self.ops = []


class SemPool:
    def __init__(self, nc, es):
        self.nc, self.es, self.n = nc, es, 0

    def pop(self):
        self.n += 1
        return self.es.enter_context(self.nc.semaphore(f"sem{self.n}"))


class Sched:
    ENGS = ("pe", "act", "dve", "pool", "sp")

    def __init__(self, nc, sem_pool):
        self.nc = nc
        self.sem_pool = sem_pool
        self.engs = {n: Eng(n, sem_pool.pop()) for n in self.ENGS}
        self.slots = []

    def res(self, name):
        return Res(name)

    def slot(self, name):
        r = Res(name)
        r.sem = self.sem_pool.pop()
        self.slots.append(r)
        return r

    def _need(self, eng, ev, waits):
        kind, key, val = ev
        if kind == "e":
            if key is eng:
                if eng.name in ("pe", "sp") or not SAME_ENGINE_SYNC:
                    return
            sem = key.sem
        else:
            sem = key.sem
        k = id(sem)
        if eng.seen.get(k, 0) >= val:
            return
        eng.seen[k] = val
        waits.append((sem, val))

    def op(self, engname, fn, reads=(), writes=(), dma=None):
        eng = self.engs[engname]
        waits = []
        for r in reads:
            if r.w is not None:
                self._need(eng, r.w, waits)
        for w in writes:
            if w.w is not None:
                self._need(eng, w.w, waits)
            for ev in w.rs:
                self._need(eng, ev, waits)
        if dma is None:
            eng.count += 1
            ev = ("e", eng, eng.count)
            inc = (eng.sem, 1)
        else:
            dma.dcnt += 1
            ev = ("d", dma, 16 * dma.dcnt)
            inc = (dma.sem, 16)
        for r in reads:
            r.rs.append(ev)
        for w in writes:
            w.w = ev
            w.rs = []
        eng.ops.append((waits, fn, inc))

    def pe(self, fn, reads=(), writes=()):
        self.op("pe", fn, reads, writes)

    def act(self, fn, reads=(), writes=()):
        self.op("act", fn, reads, writes)

    def dve(self, fn, reads=(), writes=()):
        self.op("dve", fn, reads, writes)

    def pool(self, fn, reads=(), writes=()):
        self.op("pool", fn, reads, writes)

    def dma(self, q, fn, slot, reads=(), writes=()):
        self.op(q, fn, reads, writes, dma=slot)

    def emit(self):
        finals = []
        for e in self.engs.values():
            if e.count > 0:
                finals.append((e.sem, e.count))
        for s in self.slots:
            if s.dcnt > 0:
                finals.append((s.sem, 16 * s.dcnt))
        nc = self.nc
        with nc.Block() as block:
            def run(e, eng):
                for waits, fn, inc in eng.ops:
                    for sem, val in waits:
                        e.wait_ge(sem, val)
                    ins = fn(e)
                    ins.then_inc(inc[0], inc[1])
                for sem, val in finals:
                    e.wait_ge(sem, val)

            @block.tensor
            def _(e):
                run(e, self.engs["pe"])

            @block.scalar
            def _(e):
                run(e, self.engs["act"])

            @block.vector
            def _(e):
                run(e, self.engs["dve"])

            @block.gpsimd
            def _(e):
                run(e, self.engs["pool"])

            @block.sync
            def _(e):
                run(e, self.engs["sp"])


def sb(es, nc, name, shape, dt):
    return es.enter_context(nc.sbuf_tensor(name, list(shape), dt))


def ps(es, nc, name, shape, dt):
    return es.enter_context(nc.psum_tensor(name, list(shape), dt))


def load_w_bf16(s, W_sb, w_dram, K, slot, res):
    for c in range(K // 128):
        s.dma("pool", lambda e, c=c: e.dma_start(out=W_sb[:, c, :], in_=w_dram[c * 128:(c + 1) * 128, :]),
              slot, writes=[res])


def rms_rstd(s, ss, lnv, rstd, n, r_ss, r_ln, r_rstd):
    s.act(lambda e: e.activation(out=lnv, in_=ss, func=AF.Ln, bias=EPS, scale=1.0 / n),
          reads=[r_ss], writes=[r_ln])
    s.act(lambda e: e.activation(out=rstd, in_=lnv, func=AF.Exp, scale=-0.5),
          reads=[r_ln], writes=[r_rstd])


class NormCtx:
    def __init__(self, es, nc, s, tag, gain_dram, ident):
        self.nc, self.s = nc, s
        self.junk = sb(es, nc, tag + "junk", [128, D], BF16)
        self.gain = sb(es, nc, tag + "gain", [128, D], F32)
        self.ss = [sb(es, nc, f"{tag}ss{i}", [128, 1], F32) for i in range(2)]
        self.lnv = [sb(es, nc, f"{tag}lnv{i}", [128, 1], F32) for i in range(2)]
        self.rstd = [sb(es, nc, f"{tag}rstd{i}", [128, 1], F32) for i in range(2)]
        self.hb = [sb(es, nc, f"{tag}hb{i}", [128, D], BF16) for i in range(2)]
        self.ident = ident
        self.r_junk, self.r_gain = s.res(tag + "junk"), s.res(tag + "gain")
        self.r_ss = [s.res(f"{tag}ss{i}") for i in range(2)]
        self.r_ln = [s.res(f"{tag}ln{i}") for i in range(2)]
        self.r_rstd = [s.res(f"{tag}rstd{i}") for i in range(2)]
        self.r_hb = [s.res(f"{tag}hb{i}") for i in range(2)]
        gslot = s.slot(tag + "gslot")
        s.dma("sp", lambda e: e.dma_start(out=self.gain[:], in_=gain_dram.broadcast_to([128, D])),
              gslot, writes=[self.r_gain])

    def part1(self, k, xt, r_xt):
        s = self.s
        k = k % 2
        s.act(lambda e: e.activation(out=self.junk[:], in_=xt, func=AF.Square, accum_out=self.ss[k][:]),
              reads=[r_xt], writes=[self.r_junk, self.r_ss[k]])
        rms_rstd(s, self.ss[k][:], self.lnv[k][:], self.rstd[k][:], D, self.r_ss[k], self.r_ln[k], self.r_rstd[k])
        s.dve(lambda e: e.scalar_tensor_tensor(out=self.hb[k][:], in0=xt, scalar=self.rstd[k][:, 0:1], in1=self.gain[:],
                                               op0=ALU.mult, op1=ALU.mult),
              reads=[r_xt, self.r_rstd[k], self.r_gain], writes=[self.r_hb[k]])

    def part2(self, k, psT, r_psT, hT_dst, r_hT):
        s = self.s
        k = k % 2
        for c in range(8):
            s.pe(lambda e, c=c: e.transpose(out=psT[:, c, :], in_=self.hb[k][:, c * 128:(c + 1) * 128],
                                            identity=self.ident[:]),
                 reads=[self.r_hb[k]], writes=[r_psT])
        s.act(lambda e: e.copy(out=hT_dst, in_=psT[:]), reads=[r_psT], writes=[r_hT])

    def run(self, xt, r_xt, psT, r_psT, hT_dst, r_hT):
        self.part1(0, xt, r_xt)
        self.part2(0, psT, r_psT, hT_dst, r_hT)


def phase_A(nc, sem_pool, S, x_d, prm, cst, qT_d, kT_d, v_d, mixT_d):
    NT = S // 128
    with ExitStack() as es:
        s = Sched(nc, sem_pool)
        ident = sb(es, nc, "A_ident", [128, 128], BF16)
        W = sb(es, nc, "A_W", [128, 8, 2048], BF16)
        PW = sb(es, nc, "A_PW", [128, 4, 128], BF16)
        PT = sb(es, nc, "A_PT", [128, 12, 128], BF16)
        qg = sb(es, nc, "A_qg", [128, 512], F32)
        kg = sb(es, nc, "A_kg", [128, 512], F32)
        psc = sb(es, nc, "A_psc", [128, 4], F32)
        r_const = s.res("const")
        cslot = s.slot("cslot")
        wslot = s.slot("wslot")
        r_W = s.res("W")
        s.dma("pool", lambda e: e.dma_start(out=ident[:], in_=cst["ident"][:, :]), cslot, writes=[r_const])
        s.dma("pool", lambda e: e.dma_start(out=PT[:], in_=cst["poolT"][:, :, :]), cslot, writes=[r_const])
        s.dma("pool", lambda e: e.dma_start(out=PW[:], in_=prm["pool_w"].rearrange("g c d -> c g d")),
              cslot, writes=[r_const])
        s.dma("pool", lambda e: e.dma_start(out=qg[:], in_=prm["qg_t"].broadcast_to([128, 512])), cslot,
              writes=[r_const])
        s.dma("pool", lambda e: e.dma_start(out=kg[:], in_=prm["kg_t"].broadcast_to([128, 512])), cslot,
              writes=[r_const])
        s.dma("pool", lambda e: e.dma_start(out=psc[:], in_=prm["pool_scale"][:, :]),
              cslot, writes=[r_const])
        load_w_bf16(s, W, prm["w_in_even"], 1024, wslot, r_W)
        s.dve(lambda e: e.tensor_scalar(out=qg[:], in0=qg[:], scalar1=0.125, scalar2=None, op0=ALU.mult),
              reads=[r_const], writes=[r_const])

        nrm = NormCtx(es, nc, s, "A_n", prm["mix_norm_even"], ident)
        xt = [sb(es, nc, f"A_xt{i}", [128, D], F32) for i in range(2)]
        xslot = [s.slot(f"xslot{i}") for i in range(2)]
        hT = sb(es, nc, "A_hT", [128, 8, 128], BF16)
        r_hT = s.res("hT")
        psT = ps(es, nc, "A_psT", [128, 8, 128], BF16)
        r_psT = s.res("psT")
        psG = [ps(es, nc, f"A_psG{g}", [128, 512], F32) for g in range(4)]
        r_psG = [s.res(f"psG{g}") for g in range(4)]
        psQK = ps(es, nc, "A_psQK", [128, 8, 128], BF16)
        r_psQK = s.res("psQK")
        psP = ps(es, nc, "A_psP", [128, 4, 128], F32)
        r_psP = s.res("psP")
        psO = ps(es, nc, "A_psO", [128, 4, 128], F32)
        r_psO = s.res("psO")

        qs = [sb(es, nc, f"A_qs{w}", [128, 512], F32) for w in range(2)]
        sq = [sb(es, nc, f"A_sq{w}", [128, 512], F32) for w in range(2)]
        ss8 = [sb(es, nc, f"A_ss8{w}", [128, 8], F32) for w in range(2)]
        ln8 = [sb(es, nc, f"A_ln8{w}", [128, 8], F32) for w in range(2)]
        r8 = [sb(es, nc, f"A_r8{w}", [128, 8], F32) for w in range(2)]
        qn = [sb(es, nc, f"A_qn{w}", [128, 2, 512], BF16) for w in range(2)]
        r_qs, r_sq, r_ss8, r_ln8, r_r8 = ([s.res(f"{n}{w}") for w in range(2)] for n in ("qs", "sq", "ss8", "ln8", "r8"))
        r_qn = [s.res(f"qn{w}") for w in range(2)]
        qTst = sb(es, nc, "A_qTst", [128, 4, 512], BF16)
        kTst = sb(es, nc, "A_kTst", [128, 4, 512], BF16)
        obst = sb(es, nc, "A_obst", [128, 4, 512], BF16)
        vst = [sb(es, nc, f"A_vst{i}", [128, 512], BF16) for i in range(2)]
        sl_q, sl_k, sl_ob = s.slot("sl_q"), s.slot("sl_k"), s.slot("sl_ob")
        sl_v = [s.slot(f"sl_v{i}") for i in range(2)]
        xb = [sb(es, nc, f"A_xb{i}", [128, 512], BF16) for i in range(2)]
        r_xb = [s.res(f"xb{i}") for i in range(2)]
        pooledT = sb(es, nc, "A_pooledT", [128, 4, 128], BF16)
        r_pooledT = s.res("pooledT")

        def a_pre(i):
            b = i % 2
            s.dma("sp", lambda e: e.dma_start(out=xt[b][:], in_=x_d[i * 128:(i + 1) * 128, :]),
                  xslot[b], writes=[xslot[b]])
            nrm.part1(i, xt[b][:], xslot[b])

        a_pre(0)
        pending = []
        for i in range(NT):
            b = i % 2
            if i + 1 < NT:
                a_pre(i + 1)
            nrm.part2(i, psT, r_psT, hT[:], r_hT)
            for g in range(4):
                for c in range(8):
                    s.pe(lambda e, g=g, c=c: e.matmul(psG[g][:], lhsT=hT[:, c, :], rhs=W[:, c, g * 512:(g + 1) * 512],
                                                      start=(c == 0), stop=(c == 7)),
                         reads=[r_hT, r_W], writes=[r_psG[g]])
            while pending:
                pending.pop(0)()
            s.act(lambda e, b=b: e.copy(out=vst[b][:], in_=psG[2][:]), reads=[r_psG[2]], writes=[sl_v[b]])
            s.dma("sp", lambda e, i=i, b=b: e.dma_start(out=v_d[i * 128:(i + 1) * 128, :], in_=vst[b][:]), sl_v[b],
                  reads=[sl_v[b]])
            s.dve(lambda e, b=b: e.tensor_copy(out=xb[b][:], in_=psG[3][:]), reads=[r_psG[3]], writes=[r_xb[b]])
            for g in range(4):
                pc = 8 + g if i == 0 else g
                s.pe(lambda e, g=g, b=b, pc=pc, i=i: e.matmul(psP[:, g, :], lhsT=xb[b][:, g * 128:(g + 1) * 128],
                                                              rhs=PT[:, pc, :], start=True, stop=(i == 0)),
                     reads=[r_xb[b], r_const], writes=[r_psP])
                if i > 0:
                    s.pe(lambda e, g=g, b=b: e.matmul(psP[:, g, :], lhsT=xb[1 - b][:, g * 128:(g + 1) * 128],
                                                      rhs=PT[:, 4 + g, :], start=False, stop=True),
                         reads=[r_xb[1 - b], r_const], writes=[r_psP])
            for which, gt in enumerate((qg, kg)):
                s.act(lambda e, which=which: e.copy(out=qs[which][:], in_=psG[which][:]), reads=[r_psG[which]],
                      writes=[r_qs[which]])
            for which in range(2):
                s.dve(lambda e, which=which: e.tensor_tensor(out=sq[which][:], in0=qs[which][:], in1=qs[which][:],
                                                             op=ALU.mult), reads=[r_qs[which]], writes=[r_sq[which]])
            for which in range(2):
                s.dve(lambda e, which=which: e.tensor_reduce(out=ss8[which][:],
                                                             in_=sq[which][:].rearrange("p (a b) -> p a b", a=8),
                                                             axis=AX.X, op=ALU.add), reads=[r_sq[which]],
                      writes=[r_ss8[which]])
            for which in range(2):
                rms_rstd(s, ss8[which][:], ln8[which][:], r8[which][:], 64, r_ss8[which], r_ln8[which], r_r8[which])
            s.act(lambda e: e.copy(out=pooledT[:], in_=psP[:]), reads=[r_psP], writes=[r_pooledT])
            for g in range(4):
                s.pe(lambda e, g=g: e.matmul(psO[:, g, :], lhsT=PW[:, g, :], rhs=pooledT[:, g, :], start=True, stop=True),
                     reads=[r_pooledT, r_const], writes=[r_psO])
            pq = i % 2
            for which, gt in enumerate((qg, kg)):
                for h in range(8):
                    s.dve(lambda e, h=h, which=which, gt=gt, pq=pq: e.scalar_tensor_tensor(
                        out=qn[pq][:, which, h * 64:(h + 1) * 64], in0=qs[which][:, h * 64:(h + 1) * 64],
                        scalar=r8[which][:, h:h + 1], in1=gt[:, h * 64:(h + 1) * 64], op0=ALU.mult, op1=ALU.mult),
                        reads=[r_qs[which], r_r8[which], r_const], writes=[r_qn[pq]])
            j = i % 4
            for g in range(4):
                s.act(lambda e, g=g, j=j: e.activation(out=obst[:, g, j * 128:(j + 1) * 128], in_=psO[:, g, :],
                                                       func=AF.Copy, scale=psc[:, g:g + 1]),
                      reads=[r_psO, r_const], writes=[sl_ob])
            if j == 3:
                t0 = (i - 3) * 128
                s.dma("sp", lambda e, t0=t0: e.dma_start(out=mixT_d[4:8, :, t0:t0 + 512].rearrange("h p t -> p h t"),
                                                         in_=obst[:]), sl_ob, reads=[sl_ob])

            def deferred(i=i, pq=pq):
                j = i % 4
                for which, (Tst, slT) in enumerate(((qTst, sl_q), (kTst, sl_k))):
                    for hp in range(4):
                        s.pe(lambda e, hp=hp, which=which: e.transpose(out=psQK[:, which * 4 + hp, :],
                                                                       in_=qn[pq][:, which, hp * 128:(hp + 1) * 128],
                                                                       identity=ident[:]),
                             reads=[r_qn[pq], r_const], writes=[r_psQK])
                for which, (Tst, slT) in enumerate(((qTst, sl_q), (kTst, sl_k))):
                    s.act(lambda e, which=which, Tst=Tst: e.copy(out=Tst[:, :, j * 128:(j + 1) * 128],
                                                                 in_=psQK[:, which * 4:(which + 1) * 4, :]),
                          reads=[r_psQK], writes=[slT])
                    if j == 3:
                        dst = qT_d if which == 0 else kT_d
                        t0 = (i - 3) * 128
                        s.dma("sp", lambda e, dst=dst, Tst=Tst, t0=t0: e.dma_start(
                            out=dst[:, :, t0:t0 + 512].rearrange("h p t -> p h t"), in_=Tst[:]), slT, reads=[slT])

            pending.append(deferred)
        while pending:
            pending.pop(0)()
        s.emit()


def phase_B(nc, sem_pool, S, cst, qT_d, kT_d, v_d, mixT_d):
    NQ = S // 512
    NB = S // 128
    with ExitStack() as es:
        s = Sched(nc, sem_pool)
        mask = sb(es, nc, "B_mask", [128, 128], BF16)
        uneg = sb(es, nc, "B_uneg", [128, 128], BF16)
        oneg = sb(es, nc, "B_oneg", [128, 128], BF16)
        r_const = s.res("const")
        cslot = s.slot("cslot")
        s.dma("pool", lambda e: e.dma_start(out=mask[:], in_=cst["mask"][:, :]), cslot, writes=[r_const])
        s.dma("pool", lambda e: e.dma_start(out=uneg[:], in_=cst["uneg"][:, :]), cslot, writes=[r_const])
        s.dma("pool", lambda e: e.dma_start(out=oneg[:], in_=cst["oneg"][:, :]), cslot, writes=[r_const])
        qT = sb(es, nc, "B_qT", [128, S], BF16)
        kTh = [sb(es, nc, f"B_kT{h}", [128, S], BF16) for h in range(2)]
        vv = sb(es, nc, "B_v", [128, NB, 128], BF16)
        vp = [sb(es, nc, f"B_vp{h}", [128, NB, 128], BF16) for h in range(2)]
        r_vp = [s.res(f"vp{h}") for h in range(2)]
        oT = sb(es, nc, "B_oT", [128, S], BF16)
        sl_qT, sl_v, sl_oT = s.slot("qT"), s.slot("v"), s.slot("oT")
        sl_kTh = [s.slot(f"kT{h}") for h in range(2)]
        s.pool(lambda e: e.memset(kTh[0][64:128, :], 0.0), writes=[sl_kTh[0]])
        s.pool(lambda e: e.memset(kTh[1][0:64, :], 0.0), writes=[sl_kTh[1]])
        s.pool(lambda e: e.memset(vp[0][:, :, 64:128], 0.0), writes=[r_vp[0]])
        s.pool(lambda e: e.memset(vp[1][:, :, 0:64], 0.0), writes=[r_vp[1]])
        NR = 3
        psA = [ps(es, nc, f"B_psA{i}", [128, 512], F32) for i in range(NR)]
        psB = [ps(es, nc, f"B_psB{i}", [128, 512], F32) for i in range(NR)]
        psO = [ps(es, nc, f"B_psO{i}", [128, 512], F32) for i in range(2)]
        r_psA = [s.res(f"psA{i}") for i in range(NR)]
        r_psB = [s.res(f"psB{i}") for i in range(NR)]
        r_psO = [s.res(f"psO{i}") for i in range(2)]
        ee = [sb(es, nc, f"B_e{i}", [128, 512], F32) for i in range(NR)]
        sp = [sb(es, nc, f"B_sp{i}", [128, 512], BF16) for i in range(NR)]
        ww = [sb(es, nc, f"B_w{i}", [128, 512], BF16) for i in range(NR)]
        r_e = [s.res(f"e{i}") for i in range(NR)]
        r_sp = [s.res(f"sp{i}") for i in range(NR)]
        r_w = [s.res(f"w{i}") for i in range(NR)]
        ssum = [sb(es, nc, f"B_ssum{h}", [128, 512], F32) for h in range(2)]
        ssbf = [[sb(es, nc, f"B_ssbf{h}{i}", [128, 512], BF16) for i in range(2)] for h in range(2)]
        r_ssum = [s.res(f"ssum{h}") for h in range(2)]
        r_ssbf = [[s.res(f"ssbf{h}{i}") for i in range(2)] for h in range(2)]
        rr = 0
        for hp in range(4):
            s.dma("sp", lambda e, hp=hp: e.dma_start(out=qT[:], in_=qT_d[hp, :, :]), sl_qT, writes=[sl_qT])
            for h in range(2):
                s.dma("sp", lambda e, hp=hp, h=h: e.dma_start(out=kTh[h][64 * h:64 * h + 64, :],
                                                              in_=kT_d[hp, 64 * h:64 * h + 64, :]), sl_kTh[h],
                      writes=[sl_kTh[h]])
            s.dma("sp", lambda e, hp=hp: e.dma_start(
                out=vv[:], in_=v_d[:, hp * 128:(hp + 1) * 128].rearrange("(b p) f -> p b f", p=128)), sl_v,
                writes=[sl_v])
            for h in range(2):
                s.pool(lambda e, h=h: e.tensor_copy(out=vp[h][:, :, 64 * h:64 * h + 64], in_=vv[:, :, 64 * h:64 * h + 64]),
                       reads=[sl_v], writes=[r_vp[h]])
            items = []
            for qt in range(NQ):
                cnt = [0, 0]
                for kb in range(4 * qt + 3, -1, -1):
                    r = kb - 4 * qt
                    for h in range(2):
                        it = dict(qt=qt, kb=kb, h=h, r=r, q0=128 * max(r, 0), diag=(r >= 0), c=cnt[h], i=rr % NR,
                                  po=qt % 2, last=(kb == 0 and h == 1))
                        it["N"] = 512 - it["q0"]
                        rr += 1
                        cnt[h] += 1
                        items.append(it)

            def s1(it):
                i, N, q0, h, kb, qt = it["i"], it["N"], it["q0"], it["h"], it["kb"], it["qt"]
                kk = kTh[h][:, kb * 128:(kb + 1) * 128]
                qq = qT[:, qt * 512 + q0:qt * 512 + 512]
                it["kk"], it["qq"] = kk, qq
                s.pe(lambda e: e.matmul(psA[i][:, 0:N], lhsT=kk, rhs=qq, start=True, stop=True),
                     reads=[sl_qT, sl_kTh[h]], writes=[r_psA[i]])

            def s2a(it):
                i, N = it["i"], it["N"]
                s.act(lambda e: e.activation(out=ee[i][:, 0:N], in_=psA[i][:, 0:N], func=AF.Exp),
                      reads=[r_psA[i]], writes=[r_e[i]])

            def s2b(it):
                i, N, q0, h, c = it["i"], it["N"], it["q0"], it["h"], it["c"]
                s.act(lambda e: e.activation(out=sp[i][:, 0:N], in_=ee[i][:, 0:N], func=AF.Ln, bias=1.0),
                      reads=[r_e[i]], writes=[r_sp[i]])
                if it["diag"]:
                    s.dve(lambda e: e.tensor_tensor(out=sp[i][:, 0:128], in0=sp[i][:, 0:128], in1=mask[:], op=ALU.mult),
                          reads=[r_sp[i], r_const], writes=[r_sp[i]])
                if it["kb"] > 0:
                    if c == 0:
                        s.dve(lambda e: e.memset(ssum[h][:], 0.0), writes=[r_ssum[h]])
                    s.dve(lambda e: e.tensor_tensor(out=ssum[h][:, q0:512], in0=ssum[h][:, q0:512], in1=sp[i][:, 0:N],
                                                    op=ALU.add), reads=[r_sp[i], r_ssum[h]], writes=[r_ssum[h]])
                    s.dve(lambda e: e.tensor_copy(out=ssbf[h][c % 2][:], in_=ssum[h][:]), reads=[r_ssum[h]],
                          writes=[r_ssbf[h][c % 2]])

            def s3(it):
                i, N, q0, h, c = it["i"], it["N"], it["q0"], it["h"], it["c"]
                kk, qq = it["kk"], it["qq"]
                s.pe(lambda e: e.matmul(psB[i][:, 0:N], lhsT=kk, rhs=qq, start=True, stop=False),
                     reads=[sl_qT, sl_kTh[h]], writes=[r_psB[i]])
                s.pe(lambda e: e.matmul(psB[i][:, 0:N], lhsT=uneg[:], rhs=sp[i][:, 0:N], start=False, stop=(c == 0)),
                     reads=[r_sp[i], r_const], writes=[r_psB[i]])
                if c > 0:
                    sbf = ssbf[h][(c - 1) % 2]
                    s.pe(lambda e: e.matmul(psB[i][:, 0:N], lhsT=oneg[:], rhs=sbf[:, q0:512], start=False, stop=True),
                         reads=[r_ssbf[h][(c - 1) % 2], r_const], writes=[r_psB[i]])

            def s4(it):
                i, N = it["i"], it["N"]
                s.act(lambda e: e.activation(out=ww[i][:, 0:N], in_=psB[i][:, 0:N], func=AF.Exp),
                      reads=[r_psB[i]], writes=[r_w[i]])
                if it["diag"]:
                    s.dve(lambda e: e.tensor_tensor(out=ww[i][:, 0:128], in0=ww[i][:, 0:128], in1=mask[:], op=ALU.mult),
                          reads=[r_w[i], r_const], writes=[r_w[i]])

            def s5(it):
                i, N, q0, kb, h, po, qt = it["i"], it["N"], it["q0"], it["kb"], it["h"], it["po"], it["qt"]
                s.pe(lambda e: e.matmul(psO[po][:, q0:512], lhsT=vp[h][:, kb, :], rhs=ww[i][:, 0:N],
                                        start=(it["c"] == 0 and h == 0), stop=(kb == 0 and h == 1),
                                        skip_group_check=True),
                     reads=[r_w[i], r_vp[h]], writes=[r_psO[po]])
                if it["last"]:
                    s.act(lambda e: e.copy(out=oT[:, qt * 512:(qt + 1) * 512], in_=psO[po][:]), reads=[r_psO[po]],
                          writes=[sl_oT])

            n = len(items)
            for t in range(n + 4):
                if t < n:
                    s1(items[t])
                if 0 <= t - 2 < n:
                    s3(items[t - 2])
                if 0 <= t - 4 < n:
                    s5(items[t - 4])
                if 0 <= t - 1 < n:
                    s2a(items[t - 1])
                if 0 <= t - 3 < n:
                    s4(items[t - 3])
                if 0 <= t - 1 < n:
                    s2b(items[t - 1])
            s.dma("sp", lambda e, hp=hp: e.dma_start(out=mixT_d[hp, :, :], in_=oT[:]), sl_oT, reads=[sl_oT])
        s.emit()


def phase_P(nc, sem_pool, S, x_d, mixT_d, w_out, xmid_d, tag):
    NQ = S // 512
    with ExitStack() as es:
        s = Sched(nc, sem_pool)
        W = sb(es, nc, tag + "W", [128, 8, 1024], BF16)
        r_W = s.res("W")
        wslot = s.slot("wslot")
        load_w_bf16(s, W, w_out, 1024, wslot, r_W)
        mt = [sb(es, nc, f"{tag}mt{i}", [128, 8, 512], BF16) for i in range(2)]
        sl_mt = [s.slot(f"mt{i}") for i in range(2)]
        xt = [sb(es, nc, f"{tag}xt{i}", [128, D], F32) for i in range(3)]
        sl_x = [s.slot(f"x{i}") for i in range(3)]
        psY = [ps(es, nc, f"{tag}psY{i}", [128, 512], F32) for i in range(4)]
        r_psY = [s.res(f"psY{i}") for i in range(4)]
        k = 0
        for qt in range(NQ):
            b = qt % 2
            s.dma("sp", lambda e, qt=qt, b=b: e.dma_start(
                out=mt[b][:], in_=mixT_d[:, :, qt * 512:(qt + 1) * 512].rearrange("h p t -> p h t")), sl_mt[b],
                writes=[sl_mt[b]])
            for st in range(4):
                t0 = qt * 512 + st * 128
                xb_ = k % 3
                pb = (k % 2) * 2
                k += 1
                s.dma("sp", lambda e, t0=t0, xb_=xb_: e.dma_start(out=xt[xb_][:], in_=x_d[t0:t0 + 128, :]), sl_x[xb_],
                      writes=[sl_x[xb_]])
                for n in range(2):
                    for c in range(8):
                        s.pe(lambda e, n=n, c=c, b=b, st=st, pb=pb: e.matmul(
                            psY[pb + n][:], lhsT=mt[b][:, c, st * 128:(st + 1) * 128], rhs=W[:, c, n * 512:(n + 1) * 512],
                            start=(c == 0), stop=(c == 7)), reads=[sl_mt[b], r_W], writes=[r_psY[pb + n]])
                    s.dve(lambda e, n=n, xb_=xb_, pb=pb: e.tensor_tensor(
                        out=xt[xb_][:, n * 512:(n + 1) * 512], in0=xt[xb_][:, n * 512:(n + 1) * 512], in1=psY[pb + n][:],
                        op=ALU.add), reads=[r_psY[pb + n], sl_x[xb_]], writes=[sl_x[xb_]])
                s.dma("sp", lambda e, t0=t0, xb_=xb_: e.dma_start(out=xmid_d[t0:t0 + 128, :], in_=xt[xb_][:]), sl_x[xb_],
                      reads=[sl_x[xb_]])
        s.emit()


def phase_F(nc, sem_pool, xm_d, out_d, qtiles, gain_d, w_up, conv_w, conv_b, w_down, cst, tag):
    NJ = FH // 128
    with ExitStack() as es:
        s = Sched(nc, sem_pool)
        ident = sb(es, nc, tag + "ident", [128, 128], BF16)
        r_const = s.res("const")
        cslot = s.slot("cslot")
        s.dma("pool", lambda e: e.dma_start(out=ident[:], in_=cst["ident"][:, :]), cslot, writes=[r_const])
        WU = sb(es, nc, tag + "WU", [128, 8, 2 * FH], BF16)
        r_WU = s.res("WU")
        wslot = s.slot("wslot")
        load_w_bf16(s, WU, w_up, 1024, wslot, r_WU)
        cw = sb(es, nc, tag + "cw", [128, 3, 2 * NJ], F32)
        cb = sb(es, nc, tag + "cb", [128, 2 * NJ], F32)
        s.dma("pool", lambda e: e.dma_start(out=cw[:], in_=conv_w[:, :, :]), cslot, writes=[r_const])
        s.dma("pool", lambda e: e.dma_start(out=cb[:], in_=conv_b[:, :]), cslot, writes=[r_const])
        halo = sb(es, nc, tag + "halo", [128, 2 * NJ, 2], F32)
        r_halo = [s.res(f"halo{j}") for j in range(2 * NJ)]
        s.dve(lambda e: e.memset(halo[:], 0.0), writes=r_halo)
        nrm = NormCtx(es, nc, s, tag + "n", gain_d, ident)
        xt = [sb(es, nc, f"{tag}xt{i}", [128, D], F32) for i in range(4)]
        sl_x = [s.slot(f"x{i}") for i in range(4)]
        hT = [sb(es, nc, f"{tag}hT{i}", [128, 8, 512], BF16) for i in range(2)]
        r_hT = [s.res(f"hT{i}") for i in range(2)]
        mm = sb(es, nc, tag + "m", [128, NJ, 512], BF16)
        r_m = [s.res(f"m{j}") for j in range(NJ)]
        NWD = 8
        wd = [sb(es, nc, f"{tag}wd{i}", [128, 512], BF16) for i in range(NWD)]
        sl_wd = [s.slot(f"wd{i}") for i in range(NWD)]
        ub = [[sb(es, nc, f"{tag}ub{h}{i}", [128, 514], F32) for i in range(2)] for h in range(2)]
        r_ub = [[s.res(f"ub{h}{i}") for i in range(2)] for h in range(2)]
        cc = [[sb(es, nc, f"{tag}cc{h}{i}", [128, 512], F32) for i in range(2)] for h in range(2)]
        r_cc = [[s.res(f"cc{h}{i}") for i in range(2)] for h in range(2)]
        bank = [ps(es, nc, f"{tag}bank{i}", [128, 512], F32) for i in range(4)]
        r_bank = [s.res(f"bank{i}") for i in range(4)]
        psTs = [ps(es, nc, f"{tag}psT{i}", [128, 8, 128], BF16) for i in range(2)]
        r_psTs = [s.res(f"psT{i}") for i in range(2)]
        state = {"wdk": 0, "nk": 0}

        def norm_p1(ti, st):
            row0, nsub, _ = qtiles[ti]
            k = state["nk"]
            state["nk"] += 1
            xs = k % 4
            s.dma("sp", lambda e: e.dma_start(out=xt[xs][:], in_=xm_d[row0 + st * 128:row0 + (st + 1) * 128, :]),
                  sl_x[xs], writes=[sl_x[xs]])
            state["nn"] = state.get("nn", 0) + 1
            kk = state["nn"]
            nrm.part1(kk, xt[xs][:], sl_x[xs])
            return kk

        def norm_p2(ti, st, kk):
            hb_ = ti % 2
            nrm.part2(kk, psTs[kk % 2], r_psTs[kk % 2], hT[hb_][:, :, st * 128:(st + 1) * 128], r_hT[hb_])

        def norm_sub(ti, st):
            norm_p2(ti, st, norm_p1(ti, st))

        def wd_load(n, j):
            w = state["wdk"] % NWD
            state["wdk"] += 1
            s.dma("pool", lambda e: e.dma_start(out=wd[w][:], in_=w_down[j * 128:(j + 1) * 128, n * 512:(n + 1) * 512]),
                  sl_wd[w], writes=[sl_wd[w]])
            return w

        for st in range(qtiles[0][1]):
            norm_sub(0, st)
        for ti, (row0, nsub, dst0) in enumerate(qtiles):
            TW = nsub * 128
            hb_ = ti % 2
            hTt = hT[hb_]
            nxt = []
            pend = None
            if ti + 1 < len(qtiles):
                nxt = [(ti + 1, st) for st in range(qtiles[ti + 1][1])]
            pre_w = []
            for j in range(NJ):
                db = j % 2
                for half in range(2):
                    jj = half * NJ + j
                    pb = 2 * db + half
                    for c in range(8):
                        s.pe(lambda e, c=c, jj=jj, pb=pb, TW=TW, hTt=hTt: e.matmul(bank[pb][:, 0:TW], lhsT=WU[:, c, jj * 128:(jj + 1) * 128],
                                                                  rhs=hTt[:, c, 0:TW], start=(c == 0), stop=(c == 7)),
                             reads=[r_WU, r_hT[hb_]], writes=[r_bank[pb]])
                    u = ub[half][db]
                    o = cc[half][db]
                    ru, ro = r_ub[half][db], r_cc[half][db]
                    s.act(lambda e, u=u, pb=pb, TW=TW: e.copy(out=u[:, 2:2 + TW], in_=bank[pb][:, 0:TW]), reads=[r_bank[pb]],
                          writes=[ru])
                    s.act(lambda e, o=o, pb=pb, jj=jj, TW=TW: e.activation(out=o[:, 0:TW], in_=bank[pb][:, 0:TW], func=AF.Identity,
                                                                    scale=cw[:, 2, jj:jj + 1], bias=cb[:, jj:jj + 1]),
                          reads=[r_bank[pb], r_const], writes=[ro])
                    s.pool(lambda e, u=u, jj=jj: e.tensor_copy(out=u[:, 0:2], in_=halo[:, jj, :]), reads=[r_halo[jj]],
                           writes=[ru])
                    s.dve(lambda e, u=u, jj=jj, TW=TW: e.tensor_copy(out=halo[:, jj, :], in_=u[:, TW:TW + 2]), reads=[ru],
                          writes=[r_halo[jj]])
                    s.dve(lambda e, u=u, o=o, jj=jj, TW=TW: e.scalar_tensor_tensor(out=o[:, 0:TW], in0=u[:, 1:1 + TW],
                                                                            scalar=cw[:, 1, jj:jj + 1], in1=o[:, 0:TW],
                                                                            op0=ALU.mult, op1=ALU.add),
                          reads=[ru, r_const, ro], writes=[ro])
                for half in range(2):
                    jj = half * NJ + j
                    u = ub[half][db]
                    o = cc[half][db]
                    ru, ro = r_ub[half][db], r_cc[half][db]
                    s.dve(lambda e, u=u, o=o, jj=jj, TW=TW: e.scalar_tensor_tensor(out=o[:, 0:TW], in0=u[:, 0:TW],
                                                                            scalar=cw[:, 0, jj:jj + 1], in1=o[:, 0:TW],
                                                                            op0=ALU.mult, op1=ALU.add),
                          reads=[ru, r_const, ro], writes=[ro])
                oa, og_ = cc[0][db], cc[1][db]
                s.act(lambda e, oa=oa, j=j, TW=TW: e.activation(out=oa[:, 0:TW], in_=oa[:, 0:TW], func=AF.Silu),
                      reads=[r_cc[0][db], r_const], writes=[r_cc[0][db]])
                s.pool(lambda e, oa=oa, og_=og_, j=j, TW=TW: e.tensor_tensor(out=mm[:, j, 0:TW], in0=oa[:, 0:TW], in1=og_[:, 0:TW],
                                                                      op=ALU.mult),
                       reads=[r_cc[0][db], r_cc[1][db], r_const], writes=[r_m[j]])
                if nxt and j in (1, 6, 11, 16):
                    pend = (nxt[0], norm_p1(*nxt.pop(0)))
                if j in (4, 9, 14, 19) and pend is not None:
                    norm_p2(pend[0][0], pend[0][1], pend[1])
                    pend = None
                if j >= NJ - NWD:
                    pre_w.append(wd_load(0, len(pre_w)))
            if pend is not None:
                norm_p2(pend[0][0], pend[0][1], pend[1])
                pend = None
            while nxt:
                norm_sub(*nxt.pop(0))
            if dst0 is None:
                continue
            for n in range(2):
                for j in range(NJ):
                    if n == 0 and j < len(pre_w):
                        w = pre_w[j]
                    else:
                        w = wd_load(n, j)
                    for st in range(nsub):
                        s.pe(lambda e, j=j, st=st, w=w: e.matmul(bank[st][:], lhsT=mm[:, j, st * 128:(st + 1) * 128],
                                                                  rhs=wd[w][:], start=(j == 0), stop=(j == NJ - 1)),
                             reads=[r_m[j], sl_wd[w]], writes=[r_bank[st]])
                if n == 0:
                    xs_of = []
                    for st in range(nsub):
                        k = state["nk"]
                        state["nk"] += 1
                        xs = k % 4
                        xs_of.append(xs)
                        s.dma("sp", lambda e, xs=xs, st=st, row0=row0: e.dma_start(
                            out=xt[xs][:], in_=xm_d[row0 + st * 128:row0 + (st + 1) * 128, :]), sl_x[xs],
                            writes=[sl_x[xs]])
                for st in range(nsub):
                    xs = xs_of[st]
                    s.dve(lambda e, st=st, n=n, xs=xs: e.tensor_tensor(out=xt[xs][:, n * 512:(n + 1) * 512],
                                                                      in0=xt[xs][:, n * 512:(n + 1) * 512],
                                                                      in1=bank[st][:], op=ALU.add),
                          reads=[r_bank[st], sl_x[xs]], writes=[sl_x[xs]])
            for st in range(nsub):
                xs = xs_of[st]
                s.dma("sp", lambda e, xs=xs, st=st, dst0=dst0: e.dma_start(out=out_d[dst0 + st * 128:dst0 + (st + 1) * 128, :],
                                                               in_=xt[xs][:]), sl_x[xs], reads=[sl_x[xs]])
        s.emit()


def phase_DE(nc, sem_pool, S, x_d, xmid_d, prm, cst):
    NT = S // 128
    CQ, CK, CV, CR, CG = 0, 512, 1024, 2048, 3072
    with ExitStack() as es:
        s = Sched(nc, sem_pool)
        ident = sb(es, nc, "E_ident", [128, 128], BF16)
        m1 = sb(es, nc, "E_m1", [128, 128], F32)
        msel = sb(es, nc, "E_msel", [128, 2], F32)
        og = sb(es, nc, "E_og", [128, D], F32)
        wa2b = sb(es, nc, "E_wa2b", [17, 512], BF16)
        r_const = s.res("const")
        cslot = s.slot("cslot")
        s.dma("pool", lambda e: e.dma_start(out=ident[:], in_=cst["ident"][:, :]), cslot, writes=[r_const])
        s.dma("pool", lambda e: e.dma_start(out=m1[:], in_=cst["m1"][:, :]), cslot, writes=[r_const])
        s.dma("pool", lambda e: e.dma_start(out=msel[:], in_=cst["msel"][:, :]), cslot, writes=[r_const])
        s.dma("pool", lambda e: e.dma_start(out=og[:], in_=prm["og_t"].broadcast_to([128, D])), cslot, writes=[r_const])
        s.dma("pool", lambda e: e.dma_start(out=wa2b[0:16, :], in_=prm["gla_w_a2"][:, :]), cslot, writes=[r_const])
        s.dma("pool", lambda e: e.dma_start(out=wa2b[16:17, :], in_=prm["gla_b_a"][:, :]), cslot, writes=[r_const])
        W = sb(es, nc, "E_W", [128, 8, 3088], BF16)
        WO = sb(es, nc, "E_WO", [128, 8, 1024], BF16)
        r_W, r_WO = s.res("W"), s.res("WO")
        wslot, woslot = s.slot("wslot"), s.slot("woslot")
        load_w_bf16(s, W, prm["w_in_odd"], 1024, wslot, r_W)
        load_w_bf16(s, WO, prm["w_out_odd"], 1024, woslot, r_WO)
        nrm = NormCtx(es, nc, s, "E_n", prm["mix_norm_odd"], ident)
        xt = [sb(es, nc, f"E_xt{i}", [128, D], F32) for i in range(3)]
        sl_x = [s.slot(f"x{i}") for i in range(3)]
        hT = sb(es, nc, "E_hT", [128, 8, 128], BF16)
        r_hT = s.res("hT")
        bank = [None] + [ps(es, nc, f"E_bank{i}", [128, 512], F32) for i in range(1, 8)]
        r_bank = [s.res(f"bank{i}") for i in range(8)]
        psT0 = ps(es, nc, "E_psT0", [128, 8, 128], BF16)
        al = sb(es, nc, "E_al", [17, 128], BF16)
        r_al = s.res("al")
        s.dve(lambda e: e.memset(al[:], 1.0), writes=[r_al])
        kf = sb(es, nc, "E_kf", [128, 512], F32)
        vb = sb(es, nc, "E_vb", [128, 1024], BF16)
        sr = sb(es, nc, "E_sr", [128, 1024], F32)
        qTs = sb(es, nc, "E_qTs", [128, 4, 128], BF16)
        eg = sb(es, nc, "E_eg", [128, 512], F32)
        spg = sb(es, nc, "E_spg", [128, 512], F32)
        ed = sb(es, nc, "E_ed", [128, 512], F32)
        kdec = sb(es, nc, "E_kdec", [128, 512], BF16)
        dec = sb(es, nc, "E_dec", [128, 4, 2], F32)
        state = sb(es, nc, "E_state", [128, 4, 256], F32)
        stbf = sb(es, nc, "E_stbf", [128, 4, 256], BF16)
        osb = sb(es, nc, "E_osb", [128, D], F32)
        osq = sb(es, nc, "E_osq", [128, D], F32)
        ss4 = sb(es, nc, "E_ss4", [128, 4], F32)
        ln4 = sb(es, nc, "E_ln4", [128, 4], F32)
        r4 = sb(es, nc, "E_r4", [128, 4], F32)
        gated = sb(es, nc, "E_gated", [128, D], BF16)
        goT = sb(es, nc, "E_goT", [128, 8, 128], BF16)
        (r_kf, r_vb, r_sr, r_qTs, r_eg, r_spg, r_ed, r_kdec, r_dec, r_osb, r_osq, r_ss4, r_ln4, r_r4, r_gated,
         r_goT) = (s.res(n) for n in ("kf", "vb", "sr", "qTs", "eg", "spg", "ed", "kdec", "dec", "osb", "osq", "ss4",
                                      "ln4", "r4", "gated", "goT"))
        r_kv = [s.res(f"kv{h}") for h in range(4)]
        r_state = [s.res(f"state{h}") for h in range(4)]
        r_stbf = [s.res(f"stbf{h}") for h in range(4)]
        s.dve(lambda e: e.memset(state[:], 0.0), writes=r_state)

        def proj_tok(pb, col0):
            for c in range(8):
                s.pe(lambda e, c=c: e.matmul(bank[pb][:], lhsT=hT[:, c, :], rhs=W[:, c, col0:col0 + 512], start=(c == 0),
                                             stop=(c == 7)), reads=[r_hT, r_W], writes=[r_bank[pb]])

        def de_pre(i):
            b = i % 3
            s.dma("sp", lambda e: e.dma_start(out=xt[b][:], in_=x_d[i * 128:(i + 1) * 128, :]), sl_x[b],
                  writes=[sl_x[b]])
            nrm.part1(i, xt[b][:], sl_x[b])

        de_pre(0)
        for i in range(NT):
            b = i % 3
            nrm.part2(i, psT0, r_bank[0], hT[:], r_hT)
            proj_tok(1, CK)
            proj_tok(2, CV)
            proj_tok(3, CV + 512)
            proj_tok(4, CR)
            proj_tok(5, CR + 512)
            if i + 1 < NT:
                de_pre(i + 1)
            for h in range(4):
                for c in range(8):
                    s.pe(lambda e, h=h, c=c: e.matmul(bank[6][:, h * 128:(h + 1) * 128],
                                                      lhsT=W[:, c, CQ + h * 128:CQ + (h + 1) * 128], rhs=hT[:, c, :],
                                                      start=(c == 0), stop=(c == 7)),
                         reads=[r_hT, r_W], writes=[r_bank[6]])
            for c in range(8):
                s.pe(lambda e, c=c: e.matmul(bank[7][0:16, 0:128], lhsT=W[:, c, CG:CG + 16], rhs=hT[:, c, :],
                                             start=(c == 0), stop=(c == 7)), reads=[r_hT, r_W], writes=[r_bank[7]])
            s.act(lambda e: e.copy(out=kf[:], in_=bank[1][:]), reads=[r_bank[1]], writes=[r_kf])
            s.dve(lambda e: e.tensor_copy(out=vb[:, 0:512], in_=bank[2][:]), reads=[r_bank[2]], writes=[r_vb])
            s.dve(lambda e: e.tensor_copy(out=vb[:, 512:1024], in_=bank[3][:]), reads=[r_bank[3]], writes=[r_vb])
            s.act(lambda e: e.activation(out=sr[:, 0:512], in_=bank[4][:], func=AF.Silu), reads=[r_bank[4]],
                  writes=[r_sr])
            s.act(lambda e: e.activation(out=sr[:, 512:1024], in_=bank[5][:], func=AF.Silu), reads=[r_bank[5]],
                  writes=[r_sr])
            s.act(lambda e: e.activation(out=qTs[:], in_=bank[6][:].rearrange("p (h t) -> p h t", h=4), func=AF.Copy,
                                         scale=float(128 ** -0.5)), reads=[r_bank[6]], writes=[r_qTs])
            s.dve(lambda e: e.tensor_copy(out=al[0:16, :], in_=bank[7][0:16, 0:128]), reads=[r_bank[7]], writes=[r_al])
            s.pe(lambda e: e.matmul(bank[1][:], lhsT=al[:], rhs=wa2b[:], start=True, stop=True),
                 reads=[r_al, r_const], writes=[r_bank[1]])
            s.act(lambda e: e.activation(out=eg[:], in_=bank[1][:], func=AF.Exp, scale=-1.0), reads=[r_bank[1]],
                  writes=[r_eg])
            s.act(lambda e: e.activation(out=spg[:], in_=eg[:], func=AF.Ln, bias=1.0), reads=[r_eg], writes=[r_spg])
            s.pe(lambda e: e.matmul(bank[7][:], lhsT=m1[:], rhs=spg[:], start=True, stop=True),
                 reads=[r_spg, r_const], writes=[r_bank[7]])
            for h in range(4):
                s.pe(lambda e, h=h: e.matmul(bank[6][:, h * 2:h * 2 + 2], lhsT=spg[:, h * 128:(h + 1) * 128], rhs=msel[:],
                                             start=True, stop=True), reads=[r_spg, r_const], writes=[r_bank[6]])
            s.act(lambda e: e.activation(out=ed[:], in_=bank[7][:], func=AF.Exp), reads=[r_bank[7]], writes=[r_ed])
            s.act(lambda e: e.activation(out=dec[:], in_=bank[6][:, 0:8].rearrange("p (h c) -> p h c", h=4), func=AF.Exp),
                  reads=[r_bank[6]], writes=[r_dec])
            s.dve(lambda e: e.tensor_tensor(out=kdec[:], in0=kf[:], in1=ed[:], op=ALU.mult), reads=[r_kf, r_ed],
                  writes=[r_kdec])
            for c2 in range(2):
                cs = slice(64 * c2, 64 * c2 + 64)
                for h in range(4):
                    pk = 4 + (h // 2)
                    ko = (h % 2) * 256
                    s.pe(lambda e, h=h, cs=cs, pk=pk, ko=ko: e.matmul(bank[pk][:, ko:ko + 256],
                                                                     lhsT=kdec[cs, h * 128:(h + 1) * 128],
                                                                     rhs=vb[cs, h * 256:(h + 1) * 256], start=True,
                                                                     stop=True),
                         reads=[r_kdec, r_vb], writes=[r_kv[h], r_bank[pk]])
                for h in range(4):
                    pk = 4 + (h // 2)
                    ko = (h % 2) * 256
                    s.dve(lambda e, h=h, c2=c2, pk=pk, ko=ko: e.scalar_tensor_tensor(
                        out=state[:, h, :], in0=state[:, h, :], scalar=dec[:, h, c2:c2 + 1], in1=bank[pk][:, ko:ko + 256],
                        op0=ALU.mult, op1=ALU.add), reads=[r_state[h], r_dec, r_kv[h]], writes=[r_state[h]])
                    s.act(lambda e, h=h: e.copy(out=stbf[:, h, :], in_=state[:, h, :]), reads=[r_state[h]],
                          writes=[r_stbf[h]])
                for h in range(4):
                    ko = (h % 2) * 256
                    po = 2 + (h // 2)
                    s.pe(lambda e, h=h, cs=cs, po=po, ko=ko: e.matmul(bank[po][cs, ko:ko + 256], lhsT=qTs[:, h, cs],
                                                                     rhs=stbf[:, h, :], start=True, stop=True),
                         reads=[r_qTs, r_stbf[h]], writes=[r_bank[po]])
            s.act(lambda e: e.copy(out=osb[:, 0:512], in_=bank[2][:]), reads=[r_bank[2]], writes=[r_osb])
            s.act(lambda e: e.copy(out=osb[:, 512:1024], in_=bank[3][:]), reads=[r_bank[3]], writes=[r_osb])
            s.dve(lambda e: e.tensor_tensor(out=osq[:], in0=osb[:], in1=osb[:], op=ALU.mult), reads=[r_osb],
                  writes=[r_osq])
            s.dve(lambda e: e.tensor_reduce(out=ss4[:], in_=osq[:].rearrange("p (a b) -> p a b", a=4), axis=AX.X,
                                            op=ALU.add), reads=[r_osq], writes=[r_ss4])
            rms_rstd(s, ss4[:], ln4[:], r4[:], 256, r_ss4, r_ln4, r_r4)
            for h in range(4):
                hs = slice(h * 256, (h + 1) * 256)
                s.dve(lambda e, h=h, hs=hs: e.scalar_tensor_tensor(out=osb[:, hs], in0=osb[:, hs], scalar=r4[:, h:h + 1],
                                                                   in1=og[:, hs], op0=ALU.mult, op1=ALU.mult),
                      reads=[r_osb, r_r4, r_const], writes=[r_osb])
            s.dve(lambda e: e.tensor_tensor(out=gated[:], in0=osb[:], in1=sr[:], op=ALU.mult), reads=[r_osb, r_sr],
                  writes=[r_gated])
            for c in range(8):
                s.pe(lambda e, c=c: e.transpose(out=psT0[:, c, :], in_=gated[:, c * 128:(c + 1) * 128],
                                                identity=ident[:]), reads=[r_gated, r_const],
                     writes=[r_bank[0]])
            s.act(lambda e: e.copy(out=goT[:], in_=psT0[:]), reads=[r_bank[0]], writes=[r_goT])
            for n in range(2):
                for c in range(8):
                    s.pe(lambda e, n=n, c=c: e.matmul(bank[4 + n][:], lhsT=goT[:, c, :], rhs=WO[:, c, n * 512:(n + 1) * 512],
                                                      start=(c == 0), stop=(c == 7)), reads=[r_goT, r_WO],
                         writes=[r_bank[4 + n]])
                s.dve(lambda e, n=n, b=b: e.tensor_tensor(out=xt[b][:, n * 512:(n + 1) * 512],
                                                          in0=xt[b][:, n * 512:(n + 1) * 512], in1=bank[4 + n][:],
                                                          op=ALU.add), reads=[r_bank[4 + n], sl_x[b]], writes=[sl_x[b]])
            s.dma("sp", lambda e, i=i, b=b: e.dma_start(out=xmid_d[i * 128:(i + 1) * 128, :], in_=xt[b][:]), sl_x[b],
                  reads=[sl_x[b]])
        s.emit()


def phase_S(nc, sem_pool, S, xin_d, xsel_d, flags_d):
    H = S // 2
    with ExitStack() as es:
        s = Sched(nc, sem_pool)
        fl = sb(es, nc, "S_fl", [128, 2], F32)
        r_const = s.res("const")
        cslot = s.slot("cslot")
        s.dma("pool", lambda e: e.dma_start(out=fl[:], in_=flags_d[:, :]), cslot, writes=[r_const])
        NB_ = 3
        xa = [sb(es, nc, f"S_xa{i}", [128, D], F32) for i in range(NB_)]
        xb_ = [sb(es, nc, f"S_xb{i}", [128, D], F32) for i in range(NB_)]
        sl_a = [s.slot(f"a{i}") for i in range(NB_)]
        sl_b = [s.slot(f"b{i}") for i in range(NB_)]
        s.dma("sp", lambda e: e.dma_start(out=xa[0][:], in_=xin_d[H - 128:H, :]), sl_a[0], writes=[sl_a[0]])
        s.act(lambda e: e.activation(out=xa[0][:], in_=xa[0][:], func=AF.Copy, scale=fl[:, 1:2]),
              reads=[sl_a[0], r_const], writes=[sl_a[0]])
        s.dma("sp", lambda e: e.dma_start(out=xsel_d[0:128, :], in_=xa[0][:]), sl_a[0], reads=[sl_a[0]])
        for i in range(H // 128):
            k = (i + 1) % NB_
            s.dma("sp", lambda e, i=i, k=k: e.dma_start(out=xa[k][:], in_=xin_d[i * 128:(i + 1) * 128, :]), sl_a[k],
                  writes=[sl_a[k]])
            s.dma("sp", lambda e, i=i, k=k: e.dma_start(out=xb_[k][:], in_=xin_d[H + i * 128:H + (i + 1) * 128, :]),
                  sl_b[k], writes=[sl_b[k]])
            s.act(lambda e, k=k: e.activation(out=xa[k][:], in_=xa[k][:], func=AF.Copy, scale=fl[:, 0:1]),
                  reads=[sl_a[k], r_const], writes=[sl_a[k]])
            s.dve(lambda e, k=k: e.scalar_tensor_tensor(out=xa[k][:], in0=xb_[k][:], scalar=fl[:, 1:2], in1=xa[k][:],
                                                        op0=ALU.mult, op1=ALU.add),
                  reads=[sl_a[k], sl_b[k], r_const], writes=[sl_a[k]])
            s.dma("sp", lambda e, i=i, k=k: e.dma_start(out=xsel_d[128 + i * 128:128 + (i + 1) * 128, :], in_=xa[k][:]),
                  sl_a[k], reads=[sl_a[k]])
        s.emit()


PARAM_SHAPES = {
    "mix_norm_even": [1, D], "w_in_even": [D, 2048], "qg_t": [1, 512], "kg_t": [1, 512],
    "pool_w": [4, 128, 128], "pool_scale": [128, 4], "w_out_even": [D, D],
    "mix_norm_odd": [1, D], "w_in_odd": [D, 3088], "gla_w_a2": [16, 512], "gla_b_a": [1, 512],
    "og_t": [1, D], "w_out_odd": [D, D],
    "ffn_norm0": [1, D], "ffn_norm1": [1, D], "ffn_w_up0": [D, 2 * FH], "ffn_w_up1": [D, 2 * FH],
    "ffn_conv_w0": [128, 3, 44], "ffn_conv_w1": [128, 3, 44], "ffn_conv_b0": [128, 44], "ffn_conv_b1": [128, 44],
    "ffn_w_down0": [FH, D], "ffn_w_down1": [FH, D],
}
CONST_SHAPES = {
    "ident": ([128, 128], np.float32), "mask": ([128, 128], np.float32), "uneg": ([128, 128], np.float32),
    "oneg": ([128, 128], np.float32), "poolT": ([128, 12, 128], np.float32), "m1": ([128, 128], np.float32),
    "msel": ([128, 2], np.float32),
}


def make_consts():
    c = {}
    j = np.arange(128)
    c["ident"] = np.eye(128, dtype=np.float32)
    c["mask"] = (j[:, None] < j[None, :]).astype(np.float32)
    c["uneg"] = -(j[:, None] >= j[None, :]).astype(np.float32)
    c["oneg"] = -np.ones((128, 128), np.float32)
    pt = np.zeros((128, 12, 128), np.float32)
    for g, w in enumerate((2, 4, 8, 16)):
        s_ = j[:, None]
        t_ = j[None, :]
        band = ((s_ <= t_) & (s_ > t_ - w)).astype(np.float32)
        pt[:, g, :] = band / w - np.eye(128, dtype=np.float32)
        prev = ((s_ - 128 <= t_) & (s_ - 128 > t_ - w)).astype(np.float32)
        pt[:, 4 + g, :] = prev / w
        cnt = np.minimum(j + 1, w).astype(np.float32)[None, :]
        pt[:, 8 + g, :] = band / cnt - np.eye(128, dtype=np.float32)
    c["poolT"] = pt
    same = (j[:, None] // 64) == (j[None, :] // 64)
    c["m1"] = (-(1.0 / 16.0) * ((j[:, None] > j[None, :]) & same)).astype(np.float32)
    ms = np.zeros((128, 2), np.float32)
    ms[:64, 0] = -1.0 / 16.0
    ms[64:, 1] = -1.0 / 16.0
    c["msel"] = ms
    return c


def build_nc(S, debug=False, n_phases=6):
    nc = bass.Bass("TRN2", target_bir_lowering=False)
    x_d = nc.dram_tensor("x", [S, D], F32, kind="ExternalInput").ap()
    prm = {k: nc.dram_tensor(k, v, F32, kind="ExternalInput").ap() for k, v in PARAM_SHAPES.items()}
    cst = {k: nc.dram_tensor("c_" + k, v[0], F32, kind="ExternalInput").ap() for k, v in CONST_SHAPES.items()}
    out_d = nc.dram_tensor("out", [S // 2, D], F32, kind="ExternalOutput").ap()
    flags_d = nc.dram_tensor("flags", [128, 2], F32, kind="ExternalInput").ap()
    kind = "ExternalOutput" if debug else "Internal"
    qT_d = nc.dram_tensor("qT_d", [4, 128, S], BF16, kind=kind).ap()
    kT_d = nc.dram_tensor("kT_d", [4, 128, S], BF16, kind=kind).ap()
    v_d = nc.dram_tensor("v_d", [S, 512], BF16, kind=kind).ap()
    mixT_d = nc.dram_tensor("mixT_d", [8, 128, S], BF16, kind=kind).ap()
    xmid_d = nc.dram_tensor("xmid_d", [S, D], F32, kind=kind).ap()
    x1_d = nc.dram_tensor("x1_d", [S, D], F32, kind=kind).ap()
    xmid1_d = nc.dram_tensor("xmid1_d", [S, D], F32, kind=kind).ap()
    xsel_d = nc.dram_tensor("xsel_d", [128 + S // 2, D], F32, kind="Internal").ap()
    with ExitStack() as ges:
        sem_pool = SemPool(nc, ges)
        if n_phases >= 1:
            phase_A(nc, sem_pool, S, x_d, prm, cst, qT_d, kT_d, v_d, mixT_d)
        if n_phases >= 2:
            phase_B(nc, sem_pool, S, cst, qT_d, kT_d, v_d, mixT_d)
        if n_phases >= 3:
            phase_P(nc, sem_pool, S, x_d, mixT_d, prm["w_out_even"], xmid_d, "P0")
        if n_phases >= 4:
            qt_all = [(q * 512, 4, q * 512) for q in range(S // 512)]
            phase_F(nc, sem_pool, xmid_d, x1_d, qt_all, prm["ffn_norm0"], prm["ffn_w_up0"], prm["ffn_conv_w0"],
                    prm["ffn_conv_b0"], prm["ffn_w_down0"], cst, "F0")
        if n_phases >= 5:
            phase_DE(nc, sem_pool, S, x1_d, xmid1_d, prm, cst)
        if n_phases >= 6:
            phase_S(nc, sem_pool, S, xmid1_d, xsel_d, flags_d)
            qt_half = [(0, 1, None)] + [(128 + q * 512, 4, q * 512) for q in range(S // 1024)]
            phase_F(nc, sem_pool, xsel_d, out_d, qt_half, prm["ffn_norm1"], prm["ffn_w_up1"], prm["ffn_conv_w1"],
                    prm["ffn_conv_b1"], prm["ffn_w_down1"], cst, "F1")
    return nc


def make_param_maps(inp):
    f = lambda a: np.ascontiguousarray(np.asarray(a, dtype=np.float32))
    p = {
        "mix_norm_even": f(inp["mix_norm_even"][0][None, :]), "w_in_even": f(inp["w_in_even"][0]),
        "qg_t": f(np.tile(np.asarray(inp["sb_q_gain"][0]), 8)[None, :]),
        "kg_t": f(np.tile(np.asarray(inp["sb_k_gain"][0]), 8)[None, :]),
        "pool_w": f(inp["pool_w"][0]), "pool_scale": f(np.asarray(inp["pool_scale"][0]).reshape(4, 128).T), "w_out_even": f(inp["w_out_even"][0]),
        "mix_norm_odd": f(inp["mix_norm_odd"][0][None, :]), "w_in_odd": f(inp["w_in_odd"][0]),
        "gla_w_a2": f(inp["gla_w_a2"][0]), "gla_b_a": f(inp["gla_b_a"][0][None, :]),
        "og_t": f(np.tile(np.asarray(inp["gla_o_gain"][0]), 4)[None, :]), "w_out_odd": f(inp["w_out_odd"][0]),
    }
    for l in range(2):
        p[f"ffn_norm{l}"] = f(inp["ffn_norm"][l][None, :])
        p[f"ffn_w_up{l}"] = f(inp["ffn_w_up"][l])
        p[f"ffn_conv_w{l}"] = f(np.asarray(inp["ffn_conv_w"][l]).reshape(3, 44, 128).transpose(2, 0, 1))
        p[f"ffn_conv_b{l}"] = f(np.asarray(inp["ffn_conv_b"][l]).reshape(44, 128).T)
        p[f"ffn_w_down{l}"] = f(inp["ffn_w_down"][l])
    for k, v in make_consts().items():
        p["c_" + k] = v
    return p


def kernel(**inputs):
    x = np.asarray(inputs["x"], dtype=np.float32)
    B, S, _ = x.shape
    nc = build_nc(S)
    pm = make_param_maps(inputs)
    n_cores = 8
    in_maps = []
    for c in range(n_cores):
        m = dict(pm)
        m["x"] = np.ascontiguousarray(x[(c // 2) % B])
        fl = np.zeros((128, 2), np.float32)
        fl[:, c % 2] = 1.0
        m["flags"] = fl
        in_maps.append(m)
    res = run_bass_kernel_spmd(nc, in_maps, core_ids=list(range(n_cores)))
    out = np.stack([np.concatenate([np.asarray(res.results[2 * b]["out"], dtype=np.float32),
                                    np.asarray(res.results[2 * b + 1]["out"], dtype=np.float32)], axis=0)
                    for b in range(B)], axis=0)
    return out
```
